# Optimizing a Trainium2 kernel written in Bass

```python
import jax, jax.numpy as jnp
from jax import lax
import numpy as np

D_MODEL = 2048
BATCH = 8
SEQ = 2048
DEPTH = 1
DEC_BATCH = 128
DEC_SEQ = 1
PAST_LEN = 8192
PAGE_SIZE = 128

N_META = 16
MIX_DIM = D_MODEL
ATTN_DIM = MIX_DIM // 2
CONV_DIM = MIX_DIM - ATTN_DIM
HEAD_DIM = 64
N_HEADS = ATTN_DIM // HEAD_DIM
N_KV_HEADS = 4
GQA_GROUP = N_HEADS // N_KV_HEADS
KV_DIM = N_KV_HEADS * HEAD_DIM
WINDOW = 128
BLOCK = WINDOW
CONV_WIDTH = 3
D_FF = 5632
IN_DIM = ATTN_DIM + 2 * KV_DIM + 3 * CONV_DIM
SPLITS = [ATTN_DIM, ATTN_DIM + KV_DIM, ATTN_DIM + 2 * KV_DIM,
          ATTN_DIM + 2 * KV_DIM + CONV_DIM, ATTN_DIM + 2 * KV_DIM + 2 * CONV_DIM]
EPS = 1e-5
NEG = -1e30

kernel_name = 'hymba_swa_sink_shortconv_macaron_step'


def rmsnorm(x, g):
    xf = x.astype(jnp.float32)
    y = xf * lax.rsqrt(jnp.mean(xf * xf, axis=-1, keepdims=True) + EPS)
    return (y * g.astype(jnp.float32)).astype(x.dtype)


def swiglu(x, w_gate, w_up, w_down):
    return (jax.nn.silu(x @ w_gate) * (x @ w_up)) @ w_down


def mixer_inputs(xn, w_in):
    z = xn @ w_in
    q, k, v, gate_b, gate_c, h = jnp.split(z, SPLITS, axis=-1)
    bsz, s = xn.shape[:2]
    q = q.reshape(bsz, s, N_KV_HEADS, GQA_GROUP, HEAD_DIM)
    k = k.reshape(bsz, s, N_KV_HEADS, HEAD_DIM)
    v = v.reshape(bsz, s, N_KV_HEADS, HEAD_DIM)
    return q, k, v, gate_c * h, gate_b


def attend_with_sinks(q, k, v, mask, sink):
    s = jnp.einsum('...qkgd,...skd->...kgqs', q, k).astype(jnp.float32) * (HEAD_DIM ** -0.5)
    s = jnp.where(mask, s, NEG)
    sink_col = jnp.broadcast_to(sink.astype(jnp.float32).reshape(N_KV_HEADS, GQA_GROUP, 1, 1),
                                s.shape[:-1] + (1,))
    p = jax.nn.softmax(jnp.concatenate([s, sink_col], axis=-1), axis=-1)[..., :-1]
    return jnp.einsum('...kgqs,...skd->...qkgd', p.astype(v.dtype), v)


def swa_prompt(q, k, v, sink):
    bsz, L = q.shape[:2]
    pad_front = (-N_META) % BLOCK
    pad_end = (-(pad_front + L)) % BLOCK
    T = pad_front + L + pad_end
    nb = T // BLOCK
    qb = jnp.pad(q, ((0, 0), (pad_front, pad_end), (0, 0), (0, 0), (0, 0))).reshape(
        bsz, nb, BLOCK, N_KV_HEADS, GQA_GROUP, HEAD_DIM)

    def band(a):
        ab = jnp.pad(a, ((0, 0), (pad_front, pad_end), (0, 0), (0, 0))).reshape(
            bsz, nb, BLOCK, N_KV_HEADS, HEAD_DIM)
        prev = jnp.pad(ab, ((0, 0), (1, 0), (0, 0), (0, 0), (0, 0)))[:, :-1]
        return jnp.concatenate([prev, ab], axis=2)

    qpos = (jnp.arange(T) - pad_front).reshape(nb, BLOCK)
    kpos = jnp.concatenate([qpos - BLOCK, qpos], axis=1)
    diff = qpos[:, :, None] - kpos[:, None, :]
    mask = (kpos[:, None, :] >= 0) & (diff >= 0) & (diff <= WINDOW)
    out = attend_with_sinks(qb, band(k), band(v), mask[None, :, None, None], sink)
    return out.reshape(bsz, T, ATTN_DIM)[:, pad_front:pad_front + L]


def swa_sample(q, k, v, cache_k, cache_v, sink):
    bsz, s = q.shape[:2]
    w = cache_k.shape[1]
    k_all = jnp.concatenate([cache_k.astype(k.dtype), k], axis=1)
    v_all = jnp.concatenate([cache_v.astype(v.dtype), v], axis=1)
    diff = (jnp.arange(s) + w)[:, None] - jnp.arange(w + s)[None, :]
    mask = (diff >= 0) & (diff <= WINDOW)
    out = attend_with_sinks(q, k_all, v_all, mask, sink)
    return out.reshape(bsz, s, ATTN_DIM), k_all[:, -w:], v_all[:, -w:]


def short_conv(hist, u, gate_b, w_conv):
    s = u.shape[1]
    u_ext = jnp.concatenate([hist.astype(u.dtype), u], axis=1)
    y = u_ext[:, 0:s] * w_conv[0]
    for j in range(1, CONV_WIDTH):
        y = y + u_ext[:, j:j + s] * w_conv[j]
    return gate_b * y, u_ext[:, -(CONV_WIDTH - 1):]


def merge(attn_out, conv_out, g_a, g_c, w_out):
    return jnp.concatenate([rmsnorm(attn_out, g_a), rmsnorm(conv_out, g_c)], axis=-1) @ w_out


def setup_inputs(seed: int = 0) -> dict:
    key = jax.random.key(seed)
    ks = jax.random.split(key, 24)
    f = jnp.float32
    nrm = lambda k, shape, scale: jax.random.normal(k, shape, f) * scale
    gain = lambda k, shape: 1.0 + 0.05 * jax.random.normal(k, shape, f)
    w_buf = min(WINDOW, PAST_LEN)
    return {
        'x_prompt': nrm(ks[0], (BATCH, SEQ, D_MODEL), 1.0),
        'x_sample': nrm(ks[1], (DEC_BATCH, DEC_SEQ, D_MODEL), 1.0),
        'cache_swa_k': nrm(ks[2], (DEPTH, DEC_BATCH, w_buf, N_KV_HEADS, HEAD_DIM), 1.0),
        'cache_swa_v': nrm(ks[3], (DEPTH, DEC_BATCH, w_buf, N_KV_HEADS, HEAD_DIM), 1.0),
        'state_conv': nrm(ks[4], (DEPTH, DEC_BATCH, CONV_WIDTH - 1, CONV_DIM), 1.0),
        'meta_tokens': nrm(ks[5], (N_META, D_MODEL), 1.0),
        'g_ffn1': gain(ks[6], (DEPTH, D_MODEL)),
        'w1_gate': nrm(ks[7], (DEPTH, D_MODEL, D_FF), D_MODEL ** -0.5),
        'w1_up': nrm(ks[8], (DEPTH, D_MODEL, D_FF), D_MODEL ** -0.5),
        'w1_down': nrm(ks[9], (DEPTH, D_FF, D_MODEL), D_FF ** -0.5),
        'g_mix': gain(ks[10], (DEPTH, D_MODEL)),
        'w_in': nrm(ks[11], (DEPTH, D_MODEL, IN_DIM), D_MODEL ** -0.5),
        'attn_sinks': nrm(ks[12], (DEPTH, N_HEADS), 1.0),
        'w_conv': nrm(ks[13], (DEPTH, CONV_WIDTH, CONV_DIM), CONV_WIDTH ** -0.5),
        'g_attn_out': gain(ks[14], (DEPTH, ATTN_DIM)),
        'g_conv_out': gain(ks[15], (DEPTH, CONV_DIM)),
        'w_out': nrm(ks[16], (DEPTH, MIX_DIM, D_MODEL), MIX_DIM ** -0.5),
        'g_ffn2': gain(ks[17], (DEPTH, D_MODEL)),
        'w2_gate': nrm(ks[18], (DEPTH, D_MODEL, D_FF), D_MODEL ** -0.5),
        'w2_up': nrm(ks[19], (DEPTH, D_MODEL, D_FF), D_MODEL ** -0.5),
        'w2_down': nrm(ks[20], (DEPTH, D_FF, D_MODEL), D_FF ** -0.5),
        'g_final': gain(ks[21], (D_MODEL,)),
    }


def reference(x_prompt, x_sample, cache_swa_k, cache_swa_v, state_conv, meta_tokens,
              g_ffn1, w1_gate, w1_up, w1_down, g_mix, w_in, attn_sinks, w_conv,
              g_attn_out, g_conv_out, w_out, g_ffn2, w2_gate, w2_up, w2_down, g_final):
    bsz = x_prompt.shape[0]
    meta = jnp.broadcast_to(meta_tokens.astype(x_prompt.dtype)[None], (bsz, N_META, D_MODEL))
    xp = jnp.concatenate([meta, x_prompt], axis=1)
    xs = x_sample
    kp_l, vp_l, cp_l, ks_l, vs_l, cs_l = [], [], [], [], [], []
    for l in range(DEPTH):
        xp = xp + 0.5 * swiglu(rmsnorm(xp, g_ffn1[l]), w1_gate[l], w1_up[l], w1_down[l])
        xs = xs + 0.5 * swiglu(rmsnorm(xs, g_ffn1[l]), w1_gate[l], w1_up[l], w1_down[l])

        qp, kp, vp, up_, bp = mixer_inputs(rmsnorm(xp, g_mix[l]), w_in[l])
        ap = swa_prompt(qp, kp, vp, attn_sinks[l])
        hist0 = jnp.zeros((bsz, CONV_WIDTH - 1, CONV_DIM), up_.dtype)
        cp_out, cp_state = short_conv(hist0, up_, bp, w_conv[l])
        xp = xp + merge(ap, cp_out, g_attn_out[l], g_conv_out[l], w_out[l])
        kp_l.append(kp[:, -WINDOW:])
        vp_l.append(vp[:, -WINDOW:])
        cp_l.append(cp_state)

        qs, kss, vss, us_, bs_ = mixer_inputs(rmsnorm(xs, g_mix[l]), w_in[l])
        as_, ks_new, vs_new = swa_sample(qs, kss, vss, cache_swa_k[l], cache_swa_v[l], attn_sinks[l])
        cs_out, cs_state = short_conv(state_conv[l], us_, bs_, w_conv[l])
        xs = xs + merge(as_, cs_out, g_attn_out[l], g_conv_out[l], w_out[l])
        ks_l.append(ks_new)
        vs_l.append(vs_new)
        cs_l.append(cs_state)

        xp = xp + 0.5 * swiglu(rmsnorm(xp, g_ffn2[l]), w2_gate[l], w2_up[l], w2_down[l])
        xs = xs + 0.5 * swiglu(rmsnorm(xs, g_ffn2[l]), w2_gate[l], w2_up[l], w2_down[l])

    y_prompt = rmsnorm(xp, g_final)[:, N_META:]
    y_sample = rmsnorm(xs, g_final)
    new_swa_k_prompt = jnp.stack(kp_l, axis=0)
    new_swa_v_prompt = jnp.stack(vp_l, axis=0)
    new_conv_prompt = jnp.stack(cp_l, axis=0)
    new_swa_k_sample = jnp.stack(ks_l, axis=0)
    new_swa_v_sample = jnp.stack(vs_l, axis=0)
    new_conv_sample = jnp.stack(cs_l, axis=0)
    return (y_prompt, y_sample, new_swa_k_prompt, new_swa_v_prompt, new_conv_prompt,
            new_swa_k_sample, new_swa_v_sample, new_conv_sample)
```

```python
import numpy as np
import concourse.bass as bass
import concourse.mybir as mybir
from concourse.bass_utils import run_bass_kernel_spmd

F32 = mybir.dt.float32
BF16 = mybir.dt.bfloat16
AF = mybir.ActivationFunctionType
ALU = mybir.AluOpType

NCORES = 8
D = 2048
KC = 16
FF = 5632
G = 4
NGRP = FF // 128 // G
NBMAX = 6
TMAX = NBMAX * 128
SUPERS = [list(range(0, 6)), list(range(6, 12)), list(range(12, 17))]
EPS = 1e-5
NR = 3
OPT = {"ffn1", "mixer", "ffn2", "d2d", "attn", "conv", "samples", "outproj", "mq", "mk", "mv", "mkt", "mhist", "mrc"}


class Op:
    __slots__ = ("eng", "fn", "deps", "kind", "sem", "target", "signal", "count", "name", "reads", "writes", "grp")


class Prog:
    def __init__(self):
        self.ops = []
        self.last_w = {}
        self.readers = {}
        self.sem_counts = {}

    def add(self, eng, fn, reads=(), writes=(), kind="c", sem=None, name=""):
        op = Op()
        op.eng, op.fn, op.kind, op.sem, op.name = eng, fn, kind, sem, name
        op.signal, op.count, op.target = False, 0, 0
        op.reads = list(reads)
        op.writes = list(writes) + [t for t in op.reads if isinstance(t, tuple) and t[0] == "ps" and t not in writes]
        op.grp = None
        op.deps = {}
        self.ops.append(op)
        return op

    def group(self, ops):
        self.ngroups = getattr(self, "ngroups", 0) + 1
        for o in ops:
            o.grp = self.ngroups

    def finalize(self):
        last_w, readers, sem_counts, groups = {}, {}, {}, {}
        for op in self.ops:
            deps = {}
            for t in op.reads:
                w = last_w.get(t)
                if w is not None:
                    deps[w] = True
            for t in op.writes:
                w = last_w.get(t)
                if w is not None and w not in deps:
                    deps[w] = False
                for r in readers.get(t, ()):
                    if r not in deps:
                        deps[r] = False
            deps.pop(op, None)
            op.deps = deps
            for t in op.reads:
                readers.setdefault(t, []).append(op)
            for t in op.writes:
                last_w[t] = op
                readers[t] = []
            if op.kind == "dma":
                c = sem_counts.get(op.sem, 0) + 16
                sem_counts[op.sem] = c
                op.target = c
                if op.grp is not None:
                    groups.setdefault(op.grp, []).append(op)
        for ops in groups.values():
            t = max(o.target for o in ops)
            for o in ops:
                o.target = t
                for d in list(o.deps):
                    if d in ops:
                        del o.deps[d]
        self.last_w = last_w

    def move_after(self, jobs, anchors):
        ops = self.ops
        taken = [(anchors[k], ops[a:b]) for (a, b, k) in jobs if k < len(anchors)]
        drop = set()
        for _, seg in taken:
            drop.update(id(o) for o in seg)
        new = []
        for i, o in enumerate(ops):
            for pos, seg in taken:
                if pos == i:
                    new.extend(seg)
            if id(o) not in drop:
                new.append(o)
        self.ops = new

    @staticmethod
    def _needs_wait(op, d, raw):
        if d.kind == "dma":
            return True
        if d.eng != op.eng:
            return True
        if op.kind == "dma":
            return True
        if op.eng == "pe":
            return False
        return True

    def emit(self, nc, block, semh):
        self.finalize()
        for op in self.ops:
            for d, raw in op.deps.items():
                if d.kind == "c" and self._needs_wait(op, d, raw):
                    d.signal = True
        cnt = {}
        for op in self.ops:
            if op.kind == "c" and op.signal:
                cnt[op.eng] = cnt.get(op.eng, 0) + 1
                op.count = cnt[op.eng]
        by_eng = {}
        for op in self.ops:
            by_eng.setdefault(op.eng, []).append(op)

        def run(e, ops):
            waited = {}
            for op in ops:
                need = {}
                for d, raw in op.deps.items():
                    if not self._needs_wait(op, d, raw):
                        continue
                    if d.kind == "dma":
                        key, val = ("d", d.sem), d.target
                    else:
                        key, val = ("e", d.eng), d.count
                    if waited.get(key, 0) < val and need.get(key, 0) < val:
                        need[key] = val
                for key, val in need.items():
                    e.wait_ge(semh(key), val)
                    waited[key] = val
                ins = op.fn(e)
                if op.kind == "dma":
                    ins.then_inc(semh(("d", op.sem)), 16)
                elif op.signal:
                    ins.then_inc(semh(("e", op.eng)), 1)

        names = {"pe": "tensor", "act": "scalar", "dve": "vector", "pool": "gpsimd", "sp": "sync"}
        for eng, ops in by_eng.items():
            getattr(block, names[eng])(lambda e, ops=ops: run(e, ops))


def build_program(dbg=None):
    nc = bass.Bass("TRN2", target_bir_lowering=False)
    P = Prog()

    def din(name, shape):
        return nc.dram_tensor(name, list(shape), F32, kind="ExternalInput").ap()

    def dout(name, shape):
        return nc.dram_tensor(name, list(shape), F32, kind="ExternalOutput").ap()

    xp = din("xp", [2048, D])
    xs = din("xs", [16, D])
    ck = din("ck", [16, 128, 256])
    cv = din("cv", [16, 128, 256])
    sc = din("sc", [16, 2, 1024])
    meta = din("meta", [16, D])
    gx = din("gx", [3, 16, 128])
    gac = din("gac", [2, 8, 128])
    wcv = din("wcv", [24, 128])
    sinks = din("sinks", [1, 16])
    gfin = din("gfin", [1, D])
    FFW = NGRP * G * 128
    w1g = din("w1g", [D, FFW]); w1u = din("w1u", [D, FFW]); w1d = din("w1d", [FFW, D])
    w2g = din("w2g", [D, FFW]); w2u = din("w2u", [D, FFW]); w2d = din("w2d", [FFW, D])
    win = din("win", [D, 4608]); wout = din("wout", [D, D])

    y_p = dout("y_p", [2048, D]); y_s = dout("y_s", [16, D])
    kp_o = dout("kp_o", [128, 256]); vp_o = dout("vp_o", [128, 256]); cp_o = dout("cp_o", [2, 1024])
    ks_o = dout("ks_o", [16, 128, 256]); vs_o = dout("vs_o", [16, 128, 256]); cs_o = dout("cs_o", [16, 2, 1024])
    dbg_aps = {}
    if dbg:
        for nm, shp in dbg.items():
            dbg_aps[nm] = dout(nm, shp)

    colview = lambda W: W.rearrange("(k p) n -> p k n", p=128)
    rowview = lambda W: W.rearrange("(c p) n -> p c n", p=128)

    from contextlib import ExitStack
    es = ExitStack()
    with es:
        def sb(name, shape, dt):
            return es.enter_context(nc.sbuf_tensor(name, list(shape), dt))

        x = sb("x", [128, NBMAX, D], F32)
        xnT = sb("xnT", [128, KC, TMAX], BF16)
        hy = sb("hy", [128, 8, TMAX], BF16)
        wd = sb("wd", [128, 2, G, D], BF16)
        ring = sb("ring", [128, NR, KC, 256], BF16)
        xn_tok = sb("xn_tok", [128, 2, D], BF16)
        sgc = sb("sgc", [128, 4, 384], F32)
        csb = sgc[:].rearrange("p a b -> p (a b)")
        qT = sb("qT", [128, 8, TMAX], BF16)
        kT = sb("kT", [128, 4, 128 + TMAX], BF16)
        vaug = sb("vaug", [128, NBMAX + 1, 4, 65], BF16)
        u = sb("u", [128, TMAX + 2], F32)
        acc = sb("acc", [128, TMAX], F32)
        ysq = sb("ysq", [128, TMAX], F32)
        uhist = sb("uhist", [128, 8, 2], F32)
        histT = sb("histT", [128, 2, 8, 16], F32)
        Pt = sb("Pt", [128, 2, 512], BF16)
        attn_tok = sb("attn_tok", [128, 1024], F32)
        kv_tok = sb("kv_tok", [128, 2, 256], F32)
        Kd = sb("Kd", [128, 2, 4, 2, 64], BF16)
        KTs = sb("KTs", [128, 2, 4, 128], BF16)
        vaugS = sb("vaugS", [128, 3, 4, 65], BF16)
        Pz = sb("Pz", [128, 2, 16, 16], BF16)
        ones_bf = sb("ones_bf", [128, 128], BF16)
        ident_bf = sb("ident_bf", [128, 128], BF16)
        ones_f = sb("ones_f", [128, 128], F32)
        ident_f = sb("ident_f", [128, 128], F32)
        zeros_bf = sb("zeros_bf", [128, 128], BF16)
        mbase = sb("mbase", [128, 4, 128], BF16)
        masks = sb("masks", [128, 3, 4, 128], BF16)
        gT = sb("gT", [128, 3, 16], F32)
        gaT = sb("gaT", [128, 2, 8], F32)
        wcT = sb("wcT", [128, 24], F32)
        esink = sb("esink", [128, 16], F32)
        st_ss = sb("st_ss", [128, 8], F32)
        st_ln = sb("st_ln", [128, 8], F32)
        st_r = sb("st_r", [128, 8], F32)
        st_ssa = sb("st_ssa", [128, 2], F32)
        st_lna = sb("st_lna", [128, 2], F32)
        st_ra = sb("st_ra", [128, 2], F32)
        den = sb("den", [128, 16], F32)
        rden = sb("rden", [128, 16], F32)
        rc = sb("rc", [128, 8], F32)
        ps = es.enter_context(nc.psum_tensor("ps", [128, 8, 512], F32))

        hT = hy[:].rearrange("p (u c) t -> p u c t", u=2)
        ygT = hy
        gfin_b = qT[:].rearrange("p a b -> p (a b)")[:, 0:2 * D].bitcast(F32)
        o_view = ps[:, 4:8, :].rearrange("p b (h w) -> p (b h) w", w=128)

        def ps_bf(bank0, nbanks):
            v = ps[:, bank0:bank0 + nbanks, :].bitcast(BF16)
            return v.rearrange("p a b -> p (a b)")

        cnt = {"ring": 0, "wd": 0, "zb": 0, "gu": 0}

        def dma(eng, out, in_, sem, reads=(), writes=(), name="", **kw):
            return P.add(eng, lambda e, out=out, in_=in_, kw=kw: e.dma_start(out=out, in_=in_, **kw),
                         reads=reads, writes=writes, kind="dma", sem=sem, name=name)

        def dbg_dump(nm, src, reads):
            if nm in dbg_aps:
                dma("sp", dbg_aps[nm], src, "dbg_" + nm, reads=reads, writes=[("dbgout", nm)])

        def ring_load(W, pieces):
            s = cnt["ring"] % NR
            cnt["ring"] += 1
            ops = []
            cv_ = colview(W)
            for (d0, w, s0) in pieces:
                ops.append(dma("pool", ring[:, s, :, d0:d0 + w], cv_[:, :, s0:s0 + w], "ring%d" % s,
                               writes=[("ring", s)]))
            P.group(ops)
            return s

        def wd_load(W, c0):
            b = cnt["wd"] % 2
            cnt["wd"] += 1
            dma("pool", wd[:, b], rowview(W)[:, c0:c0 + G, :], "wd%d" % b, writes=[("wd", b)])
            return b

        def subs_of(nb):
            return [(0, 384, 0, 3), (384, nb * 128, 3, nb)]

        def setup():
            P.add("pool", lambda e: e.memset(ones_bf[:], 1.0), writes=["ones_bf"])
            P.add("pool", lambda e: e.memset(ones_f[:], 1.0), writes=["ones_f"])
            P.add("pool", lambda e: e.memset(zeros_bf[:], 0.0), writes=["zeros_bf"])
            P.add("pool", lambda e: e.affine_select(out=ident_bf[:], in_=ones_bf[:], pattern=[[-1, 128]],
                                                    compare_op=ALU.is_equal, fill=0.0, base=0, channel_multiplier=1),
                  reads=["ones_bf"], writes=["ident_bf"])
            P.add("pool", lambda e: e.affine_select(out=ident_f[:], in_=ones_f[:], pattern=[[-1, 128]],
                                                    compare_op=ALU.is_equal, fill=0.0, base=0, channel_multiplier=1),
                  reads=["ones_f"], writes=["ident_f"])
            stg = attn_tok[:].rearrange("p (a b) -> p a b", b=128)
            ops = []
            for i in range(3):
                ops.append(dma("sp", stg[0:16, i, :], gx[i], "cst", writes=["attn_tok"]))
            for i in range(2):
                ops.append(dma("sp", stg[0:8, 3 + i, :], gac[i], "cst", writes=["attn_tok"]))
            ops.append(dma("sp", stg[0:24, 5, :], wcv, "cst", writes=["attn_tok"]))
            ops.append(dma("sp", esink[:], sinks.partition_broadcast(128).rearrange("p a b -> p (a b)"), "cst", writes=["esink"]))
            P.group(ops)
            P.add("act", lambda e: e.activation(out=esink[:], in_=esink[:], func=AF.Exp), reads=["esink"], writes=["esink"])

            def tr(e):
                ins = None
                for i in range(3):
                    ins = e.transpose(out=ps[:, 3, i * 16:(i + 1) * 16], in_=stg[0:16, i, :], identity=ident_f[0:16, 0:16])
                for i in range(2):
                    ins = e.transpose(out=ps[:, 3, 48 + i * 8:48 + (i + 1) * 8], in_=stg[0:8, 3 + i, :], identity=ident_f[0:8, 0:8])
                ins = e.transpose(out=ps[:, 3, 64:88], in_=stg[0:24, 5, :], identity=ident_f[0:24, 0:24])
                return ins
            P.add("pe", tr, reads=["attn_tok", "ident_f"], writes=[("ps", 3)])
            P.add("dve", lambda e: e.tensor_copy(out=gT[:].rearrange("p a b -> p (a b)"), in_=ps[:, 3, 0:48]),
                  reads=[("ps", 3)], writes=["gT"])
            P.add("dve", lambda e: e.tensor_copy(out=gaT[:].rearrange("p a b -> p (a b)"), in_=ps[:, 3, 48:64]),
                  reads=[("ps", 3)], writes=["gaT"])
            P.add("dve", lambda e: e.tensor_copy(out=wcT[:], in_=ps[:, 3, 64:88]),
                  reads=[("ps", 3)], writes=["wcT", "attn_tok"])

        def setup_late():
            P.add("pool", lambda e: e.affine_select(out=mbase[:, 0, :], in_=ones_bf[:], pattern=[[-1, 128]],
                                                    compare_op=ALU.is_ge, fill=0.0, base=0, channel_multiplier=1),
                  reads=["ones_bf"], writes=["mb0"])
            P.add("pool", lambda e: e.affine_select(out=mbase[:, 1, :], in_=ones_bf[:], pattern=[[1, 128]],
                                                    compare_op=ALU.is_ge, fill=0.0, base=0, channel_multiplier=-1),
                  reads=["ones_bf"], writes=["mb1"])
            P.add("pool", lambda e: e.affine_select(out=mbase[:, 2, :], in_=mbase[:, 0, :], pattern=[[0, 128]],
                                                    compare_op=ALU.is_ge, fill=0.0, base=-112, channel_multiplier=1),
                  reads=["mb0"], writes=["mb2"])
            P.add("pool", lambda e: e.affine_select(out=mbase[:, 3, :], in_=mbase[:, 1, :], pattern=[[0, 128]],
                                                    compare_op=ALU.is_ge, fill=0.0, base=-112, channel_multiplier=1),
                  reads=["mb1"], writes=["mb3"])
            P.add("pool", lambda e: e.tensor_copy(out=mbase[:, 3, 0:16], in_=ident_bf[:, 0:16]),
                  reads=["ident_bf", "mb3"], writes=["mb3"])
            plan = [(0, 0, 0), (0, 1, 0), (0, 2, 1), (0, 3, 1),
                    (1, 0, 2), (1, 1, 2), (1, 2, 1), (1, 3, 1),
                    (2, 0, 0), (2, 1, 0), (2, 2, 3), (2, 3, 3)]
            for (m, q, src) in plan:
                P.add("pool", lambda e, m=m, q=q, src=src: e.tensor_copy(out=masks[:, m, q, :], in_=mbase[:, src, :]),
                      reads=["mb%d" % src], writes=["masks"])
            P.add("pool", lambda e: e.memset(vaug[:, :, :, 64:65], 1.0), writes=["vaug_ones"])
            P.add("pool", lambda e: e.memset(vaugS[:, :, :, 64:65], 1.0), writes=["vaugS_ones"])
            P.add("pool", lambda e: e.memset(uhist[:], 0.0), writes=["uhist"])
            P.add("pool", lambda e: e.memset(u[:, 0:2], 0.0), writes=["u_hist_cols"])

        def load_x(blks):
            for j, b in enumerate(blks):
                if b == 0:
                    P.add("dve", lambda e, j=j: e.memset(x[:, j, :], 0.0), writes=[("x", j)])
                    o1 = dma("sp", x[0:16, j, :], xs, "x%d" % j, writes=[("x", j)])
                    o2 = dma("sp", x[112:128, j, :], meta, "x%d" % j, writes=[("x", j)])
                    P.group([o1, o2])
                else:
                    dma("sp", x[:, j, :], xp[(b - 1) * 128:b * 128, :], "x%d" % j,
                        reads=([("x", 2)] if j >= 3 else []), writes=[("x", j)])

        def norm_stage(gi, nb):
            ends = []
            for j in range(nb):
                norm_block(gi, j)
                ends.append(len(P.ops))
            return ends

        def ffn(Wg, Wu, Wd_, nb, after_block=None):
            subs = subs_of(nb)
            early = []
            for g in range(NGRP):
                a_l = len(P.ops)
                if g > 0:
                    wb = wd_load(Wd_, g * G)
                hb = g % 2
                for pair in range(2):
                    c0 = g * 512 + pair * 256
                    sg_ = ring_load(Wg, [(0, 256, c0)])
                    su_ = ring_load(Wu, [(0, 256, c0)])
                    if g == 0 and pair == 0:
                        wb = wd_load(Wd_, 0)
                        early.append((a_l, len(P.ops), 0))

                    def mmchain(e, slot, bank, half, t0, t1):
                        ins = None
                        for k in range(KC):
                            ins = e.matmul(ps[:, bank, 0:t1 - t0], lhsT=ring[:, slot, k, half * 128:(half + 1) * 128],
                                           rhs=xnT[:, k, t0:t1], start=(k == 0), stop=(k == KC - 1))
                        return ins
                    units = [(half, si) for si in range(2) for half in range(2)]

                    def job_g(ui):
                        half, si = units[ui]
                        t0, t1, j0, j1 = subs[si]
                        w = t1 - t0
                        par = cnt["gu"] % 2
                        cnt["gu"] += 1
                        xr = [("xnT", j) for j in range(j0, j1)]
                        P.add("pe", lambda e, s=sg_, b=par, half=half, t0=t0, t1=t1: mmchain(e, s, b, half, t0, t1),
                              reads=[("ring", sg_)] + xr, writes=[("ps", par)])
                        P.add("act", lambda e, par=par, w=w, ui=ui: e.activation(
                            out=sgc[:, ui, 0:w], in_=ps[:, par, 0:w], func=AF.Silu),
                            reads=[("ps", par)], writes=[("sg", ui)])

                    def job_u(ui):
                        half, si = units[ui]
                        t0, t1, j0, j1 = subs[si]
                        w = t1 - t0
                        fcl = pair * 2 + half
                        par = cnt["gu"] % 2
                        cnt["gu"] += 1
                        xr = [("xnT", j) for j in range(j0, j1)]
                        P.add("pe", lambda e, s=su_, b=2 + par, half=half, t0=t0, t1=t1: mmchain(e, s, b, half, t0, t1),
                              reads=[("ring", su_)] + xr, writes=[("ps", 2 + par)])
                        P.add("dve", lambda e, par=par, w=w, ui=ui, hb=hb, fcl=fcl, t0=t0, t1=t1: e.tensor_tensor(
                            out=hT[:, hb, fcl, t0:t1], in0=sgc[:, ui, 0:w], in1=ps[:, 2 + par, 0:w],
                            op=ALU.mult),
                            reads=[("sg", ui), ("ps", 2 + par)], writes=[("hy", hb * 4 + fcl, si)])
                    if g == 0 and pair == 0:
                        for kk, (fn_, ui) in enumerate(((job_g, 0), (job_g, 1), (job_u, 0), (job_u, 1))):
                            a_ = len(P.ops)
                            fn_(ui)
                            early.append((a_, a_ + 1, kk))
                        for fn_, ui in ((job_g, 2), (job_g, 3), (job_u, 2), (job_u, 3)):
                            fn_(ui)
                    else:
                        for ui in range(4):
                            job_g(ui)
                        for ui in range(4):
                            job_u(ui)
                for j in range(nb):
                    si = 0 if j < 3 else 1
                    for hf in range(2):
                        bk = 4 + 2 * hf

                        def dmm(e, j=j, hf=hf, bk=bk, hb=hb, wb=wb):
                            ins = None
                            for dt in range(2):
                                col = (2 * hf + dt) * 512
                                for c in range(G):
                                    ins = e.matmul(ps[:, bk + dt, :], lhsT=hT[:, hb, c, j * 128:(j + 1) * 128],
                                                   rhs=wd[:, wb, c, col:col + 512], start=(c == 0), stop=(c == G - 1))
                            return ins
                        P.add("pe", dmm, reads=[("hy", hb * 4 + c, si) for c in range(G)] + [("wd", wb)],
                              writes=[("ps", bk), ("ps", bk + 1)])
                        P.add("dve", lambda e, j=j, hf=hf, bk=bk: e.scalar_tensor_tensor(
                            out=x[:, j, hf * 1024:(hf + 1) * 1024],
                            in0=ps[:, bk:bk + 2, :].rearrange("p a b -> p (a b)"), scalar=0.5,
                            in1=x[:, j, hf * 1024:(hf + 1) * 1024], op0=ALU.mult, op1=ALU.add),
                            reads=[("ps", bk), ("ps", bk + 1), ("x", j)], writes=[("x", j)])
                    if g == NGRP - 1 and after_block is not None:
                        after_block(j)

            return early

        def mixer(si_, blks):
            nb = len(blks)
            T = nb * 128
            subs = subs_of(nb)
            first = (si_ == 0)
            has0 = 0 in blks
            has16 = 16 in blks
            ststg = attn_tok

            def zbank():
                b = cnt["zb"] % 4
                cnt["zb"] += 1
                return b

            def zchain(slot, half, t0, t1, bank, xr):
                w = t1 - t0

                def f(e):
                    ins = None
                    for k in range(KC):
                        ins = e.matmul(ps[:, bank, 0:w], lhsT=ring[:, slot, k, half * 128:(half + 1) * 128],
                                       rhs=xnT[:, k, t0:t1], start=(k == 0), stop=(k == KC - 1))
                    return ins
                P.add("pe", f, reads=[("ring", slot)] + xr, writes=[("ps", bank)])

            if not first:
                nbp = len(SUPERS[si_ - 1])
                P.add("act", lambda e: e.activation(out=kT[:, :, 0:128], in_=kT[:, :, nbp * 128:(nbp + 1) * 128], func=AF.Copy),
                      reads=[("kT", nbp)], writes=[("kT", 0)])
                P.add("act", lambda e: e.activation(out=vaug[:, 0, :, 0:64], in_=vaug[:, nbp, :, 0:64], func=AF.Copy),
                      reads=[("vaug", nbp)], writes=[("vaug", 0)])
            if has0 and "mhist" in OPT:
                for jj in range(2):
                    dma("sp", ststg[0:16, :], sc[:, jj, :], "cst2", reads=[], writes=["attn_tok"])

                    def trh(e):
                        ins = None
                        for cj in range(8):
                            ins = e.transpose(out=ps[:, 6, cj * 16:(cj + 1) * 16], in_=ststg[0:16, cj * 128:(cj + 1) * 128],
                                              identity=ident_f[0:16, 0:16])
                        return ins
                    P.add("pe", trh, reads=["attn_tok", "ident_f"], writes=[("ps", 6)])
                    P.add("dve", lambda e, jj=jj: e.tensor_copy(out=histT[:, jj].rearrange("p a b -> p (a b)"), in_=ps[:, 6, 0:128]),
                          reads=[("ps", 6)], writes=["histT", "attn_tok"])

            early = []
            for qs in range(4 if "mq" in OPT else 0):
                a_l = len(P.ops)
                s = ring_load(win, [(0, 256, qs * 256)])
                if qs < 1:
                    early.append((a_l, len(P.ops), 2 * qs))
                for si, (t0, t1, j0, j1) in enumerate(subs):
                    for half in range(2):
                        c8 = qs * 2 + half
                        a_ = len(P.ops)
                        bk = zbank()
                        zchain(s, half, t0, t1, bk, [("xnT", j) for j in range(j0, j1)])
                        P.add("act", lambda e, bk=bk, c8=c8, t0=t0, t1=t1: e.activation(
                            out=qT[:, c8, t0:t1], in_=ps[:, bk, 0:t1 - t0], func=AF.Copy, scale=0.125),
                            reads=[("ps", bk)], writes=[("qT", c8, si)])
                        if qs < 1 and si == 0:
                            early.append((a_, a_ + 1, 2 * qs + half))
            for ks_ in range(2 if "mk" in OPT else 0):
                s = ring_load(win, [(i * 64, 64, 1024 + (2 * ks_ + i // 2) * 64) for i in range(4)])
                for half in range(2):
                    kv = ks_ * 2 + half
                    for si, (t0, t1, j0, j1) in enumerate(subs):
                        bk = zbank()
                        zchain(s, half, t0, t1, bk, [("xnT", j) for j in range(j0, j1)])
                        P.add("act", lambda e, bk=bk, kv=kv, t0=t0, t1=t1: e.activation(
                            out=kT[:, kv, 128 + t0:128 + t1], in_=ps[:, bk, 0:t1 - t0], func=AF.Copy),
                            reads=[("ps", bk)], writes=[("kT", j + 1) for j in range(j0, j1)])
            if "mv" in OPT:
                s = ring_load(win, [(0, 256, 1280)])
            for j, b in enumerate(blks if "mv" in OPT else []):
                vb = 6 + (j % 2)

                def vmm(e, j=j, s=s, vb=vb):
                    ins = None
                    for k in range(KC):
                        ins = e.matmul(ps[:, vb, 0:256], lhsT=xnT[:, k, j * 128:(j + 1) * 128], rhs=ring[:, s, k, :],
                                       start=(k == 0), stop=(k == KC - 1))
                    return ins
                P.add("pe", vmm, reads=[("ring", s), ("xnT", j)], writes=[("ps", vb)])
                P.add("act", lambda e, j=j, vb=vb: e.activation(out=vaug[:, j + 1, :, 0:64],
                                                                in_=ps[:, vb, 0:256].rearrange("p (a b) -> p a b", a=4), func=AF.Copy),
                      reads=[("ps", vb)], writes=[("vaug", j + 1)])
                if b == 0 or b == 16:
                    P.add("dve", lambda e, vb=vb: e.tensor_copy(out=kv_tok[:, 1, :], in_=ps[:, vb, 0:256]),
                          reads=[("ps", vb)], writes=[("kv_tok", 1)])
                    if b == 16:
                        dma("sp", vp_o, kv_tok[:, 1, :], "okv1", reads=[("kv_tok", 1)], writes=[("outd", "vp")])
                    else:
                        dma("sp", vs_o[:, 127, :], kv_tok[0:16, 1, :], "okv1", reads=[("kv_tok", 1)], writes=[("outd", "vs")])
            if (has0 or has16) and "mkt" in OPT:
                s = ring_load(win, [(0, 256, 1024)])
                for j, b in enumerate(blks):
                    if b != 0 and b != 16:
                        continue

                    def kmm(e, j=j, s=s):
                        ins = None
                        for k in range(KC):
                            ins = e.matmul(ps[:, 7, 0:256], lhsT=xnT[:, k, j * 128:(j + 1) * 128], rhs=ring[:, s, k, :],
                                           start=(k == 0), stop=(k == KC - 1))
                        return ins
                    P.add("pe", kmm, reads=[("ring", s), ("xnT", j)], writes=[("ps", 7)])
                    P.add("dve", lambda e: e.tensor_copy(out=kv_tok[:, 0, :], in_=ps[:, 7, 0:256]),
                          reads=[("ps", 7)], writes=[("kv_tok", 0)])
                    if b == 16:
                        dma("sp", kp_o, kv_tok[:, 0, :], "okv0", reads=[("kv_tok", 0)], writes=[("outd", "kp")])
                    else:
                        dma("sp", ks_o[:, 127, :], kv_tok[0:16, 0, :], "okv0", reads=[("kv_tok", 0)], writes=[("outd", "ks")])
            col_of = {"C": 2560, "h": 3584, "B": 1536}
            seq = []
            for cj in range(8):
                for nm in ("C", "h", "B"):
                    seq.append((nm, cj))
            slot_of = {}
            for i in range(0, len(seq), 2):
                pieces = [(0, 128, col_of[seq[i][0]] + seq[i][1] * 128), (128, 128, col_of[seq[i + 1][0]] + seq[i + 1][1] * 128)]
                slot_of[i // 2] = pieces
            cur_slot = None
            pending_ssq = []
            if "conv" not in OPT:
                seq = []
            for idx, (nm, cj) in enumerate(seq):
                if idx % 2 == 0:
                    cur_slot = ring_load(win, slot_of[idx // 2])
                half = idx % 2
                if nm == "C":
                    for si, (t0, t1, j0, j1) in enumerate(subs):
                        bk = zbank()
                        zchain(cur_slot, half, t0, t1, bk, [("xnT", j) for j in range(j0, j1)])
                        P.add("act", lambda e, bk=bk, t0=t0, t1=t1: e.activation(out=csb[:, t0:t1], in_=ps[:, bk, 0:t1 - t0], func=AF.Copy),
                              reads=[("ps", bk)], writes=[("sg", 2 * si), ("sg", 2 * si + 1)])
                elif nm == "h":
                    while pending_ssq:
                        pending_ssq.pop(0)()
                    if not first:
                        P.add("act", lambda e, cj=cj: e.activation(out=u[:, 0:2], in_=uhist[:, cj, :], func=AF.Copy),
                              reads=["uhist"], writes=["u_hist_cols"])
                    for si, (t0, t1, j0, j1) in enumerate(subs):
                        bk = zbank()
                        zchain(cur_slot, half, t0, t1, bk, [("xnT", j) for j in range(j0, j1)])
                        P.add("dve", lambda e, bk=bk, t0=t0, t1=t1: e.tensor_tensor(
                            out=u[:, 2 + t0:2 + t1], in0=csb[:, t0:t1], in1=ps[:, bk, 0:t1 - t0], op=ALU.mult),
                            reads=[("ps", bk), ("sg", 2 * si), ("sg", 2 * si + 1)], writes=[("u", si)])
                else:
                    bks = []
                    for si, (t0, t1, j0, j1) in enumerate(subs):
                        bk = zbank()
                        bks.append(bk)
                        zchain(cur_slot, half, t0, t1, bk, [("xnT", j) for j in range(j0, j1)])
                    ur = [("u", 0), ("u", 1), "u_hist_cols"]
                    P.add("act", lambda e, cj=cj: e.activation(out=acc[:, 0:T], in_=u[:, 2:2 + T], func=AF.Copy,
                                                               scale=wcT[:, 16 + cj:17 + cj]),
                          reads=ur + ["wcT"], writes=["acc"])
                    P.add("dve", lambda e, cj=cj: e.scalar_tensor_tensor(out=acc[:, 0:T], in0=u[:, 1:1 + T], scalar=wcT[:, 8 + cj:9 + cj],
                                                                         in1=acc[:, 0:T], op0=ALU.mult, op1=ALU.add),
                          reads=ur + ["acc"], writes=["acc"])
                    P.add("dve", lambda e, cj=cj: e.scalar_tensor_tensor(out=acc[:, 0:T], in0=u[:, 0:T], scalar=wcT[:, cj:cj + 1],
                                                                         in1=acc[:, 0:T], op0=ALU.mult, op1=ALU.add),
                          reads=ur + ["acc"], writes=["acc"])
                    if has0:
                        P.add("dve", lambda e, cj=cj: e.tensor_scalar(out=acc[:, 0:16], in0=u[:, 2:18], scalar1=wcT[:, 16 + cj:17 + cj],
                                                                      scalar2=None, op0=ALU.mult),
                              reads=ur + ["acc"], writes=["acc"])
                        P.add("dve", lambda e, cj=cj: e.scalar_tensor_tensor(out=acc[:, 0:16], in0=histT[:, 1, cj, :], scalar=wcT[:, 8 + cj:9 + cj],
                                                                             in1=acc[:, 0:16], op0=ALU.mult, op1=ALU.add),
                              reads=["histT", "acc"], writes=["acc"])
                        P.add("dve", lambda e, cj=cj: e.scalar_tensor_tensor(out=acc[:, 0:16], in0=histT[:, 0, cj, :], scalar=wcT[:, cj:cj + 1],
                                                                             in1=acc[:, 0:16], op0=ALU.mult, op1=ALU.add),
                              reads=["histT", "acc"], writes=["acc"])
                    for si, (t0, t1, j0, j1) in enumerate(subs):
                        P.add("dve", lambda e, bk=bks[si], t0=t0, t1=t1: e.tensor_tensor(
                            out=acc[:, t0:t1], in0=acc[:, t0:t1], in1=ps[:, bk, 0:t1 - t0], op=ALU.mult),
                            reads=[("ps", bks[si]), "acc"], writes=["acc"])
                    if has16:
                        P.add("pe", lambda e, cj=cj: e.transpose(out=ps[:, 6, cj * 128:(cj + 1) * 128] if cj < 4 else
                                                                 ps[:, 7, (cj - 4) * 128:(cj - 3) * 128], in_=u[:, T + 2 - 128:T + 2],
                                                                 identity=ident_f[:]),
                              reads=ur + ["ident_f"], writes=[("ps", 6 if cj < 4 else 7)])
                    if has0:
                        P.add("pe", lambda e, cj=cj: e.transpose(out=ps[:, 6, cj * 128:(cj + 1) * 128] if cj < 4 else
                                                                 ps[:, 7, (cj - 4) * 128:(cj - 3) * 128], in_=u[:, 2:130],
                                                                 identity=ident_f[:]),
                              reads=ur + ["ident_f"], writes=[("ps", 6 if cj < 4 else 7)])
                    P.add("act", lambda e, cj=cj: e.activation(out=uhist[:, cj, :], in_=u[:, T:T + 2], func=AF.Copy),
                          reads=ur, writes=["uhist"])
                    P.add("act", lambda e: e.activation(out=ysq[:, 0:T], in_=acc[:, 0:T], func=AF.Square),
                          reads=["acc"], writes=["ysq"])
                    def ssq_mm(cj=cj):
                        for si, (t0, t1, j0, j1) in enumerate(subs):
                            P.add("pe", lambda e, cj=cj, si=si, t0=t0, t1=t1: e.matmul(
                                ps[:, 4 + si, 0:t1 - t0], lhsT=ones_f[:], rhs=ysq[:, t0:t1], start=(cj == 0), stop=(cj == 7)),
                                reads=["ysq", "ones_f"], writes=[("ps", 4 + si)])
                    pending_ssq.append(ssq_mm)
                    P.add("act", lambda e, cj=cj: e.activation(out=ygT[:, cj, 0:T], in_=acc[:, 0:T], func=AF.Copy,
                                                               scale=gaT[:, 1, cj:cj + 1]),
                          reads=["acc", "gaT"], writes=[("hy", cj, 0), ("hy", cj, 1)])
            while pending_ssq:
                pending_ssq.pop(0)()
            if has16 and "conv" in OPT:
                P.add("dve", lambda e: e.tensor_copy(out=attn_tok[96:128, :], in_=ps[96:128, 6:8, :].rearrange("p a b -> p (a b)")),
                      reads=[("ps", 6), ("ps", 7)], writes=["attn_tok"])
                dma("sp", cp_o, attn_tok[126:128, :], "ocv", reads=["attn_tok"], writes=[("outd", "cp")])
            if has0 and "conv" in OPT:
                P.add("dve", lambda e: e.tensor_copy(out=attn_tok[0:16, :], in_=ps[0:16, 6:8, :].rearrange("p a b -> p (a b)")),
                      reads=[("ps", 6), ("ps", 7)], writes=["attn_tok"])
                dma("sp", cs_o[:, 1, :], attn_tok[0:16, :], "ocv", reads=["attn_tok"], writes=[("outd", "cs")])
            for si, (t0, t1, j0, j1) in enumerate(subs if "mrc" in OPT else []):
                P.add("act", lambda e, si=si, t0=t0, t1=t1: e.activation(out=ysq[:, t0:t1], in_=ps[:, 4 + si, 0:t1 - t0], func=AF.Ln,
                                                                         scale=1.0 / 1024, bias=EPS),
                      reads=[("ps", 4 + si)], writes=["ysq"])
                P.add("act", lambda e, t0=t0, t1=t1: e.activation(out=ysq[:, t0:t1], in_=ysq[:, t0:t1], func=AF.Exp, scale=-0.5),
                      reads=["ysq"], writes=["ysq"])
            for j in range(nb if "mrc" in OPT else 0):
                rb = 3 - (j % 2)
                P.add("pe", lambda e, j=j, rb=rb: e.transpose(out=ps[:, rb, 0:32], in_=ysq[0:32, j * 128:(j + 1) * 128], identity=ident_f[0:32, 0:32]),
                      reads=["ysq", "ident_f"], writes=[("ps", rb)])
                P.add("dve", lambda e, j=j, rb=rb: e.tensor_copy(out=rc[:, j:j + 1], in_=ps[:, rb, 0:1]),
                      reads=[("ps", rb)], writes=[("rc", j)])

            units = [(kv, hf) for kv in range(4) for hf in range(2)]
            orr = [("ps", 4), ("ps", 5), ("ps", 6), ("ps", 7)]

            def binfo(j):
                b = blks[j]
                return ([1] if b == 0 else [0, 1]), (2 if b == 0 else (1 if b == 1 else 0)), (256 if b == 0 else 0), (b == 0)

            def emit_S(j, n):
                kv, hf = units[n]
                kbs, mi, c0m, opened = binfo(j)
                sb_ = (j * 8 + n) % 3

                def smm(e):
                    ins = None
                    for kb in kbs:
                        kc0 = (j + kb) * 128
                        ins = e.matmul(ps[:, sb_, kb * 256:(kb + 1) * 256].rearrange("p (a b) -> p a b", a=2),
                                       lhsT=kT[hf * 64:(hf + 1) * 64, kv, kc0:kc0 + 128],
                                       rhs=qT[hf * 64:(hf + 1) * 64, 2 * kv:2 * kv + 2, j * 128:(j + 1) * 128],
                                       start=True, stop=True)
                    return ins
                sj = 0 if j < 3 else 1
                P.add("pe", smm, reads=[("kT", j + kb) for kb in kbs] + [("qT", 2 * kv, sj), ("qT", 2 * kv + 1, sj)],
                      writes=[("ps", sb_)])

            def emit_expmask(j, n):
                kbs, mi, c0m, opened = binfo(j)
                sb_ = (j * 8 + n) % 3
                pb = (j * 8 + n) % 2
                P.add("act", lambda e: e.activation(out=Pt[:, pb, c0m:512], in_=ps[:, sb_, c0m:512], func=AF.Exp),
                      reads=[("ps", sb_)], writes=[("Pt", pb)])
                P.add("dve", lambda e: e.tensor_tensor(
                    out=Pt[:, pb, c0m:512], in0=Pt[:, pb, c0m:512],
                    in1=masks[:, mi].rearrange("p a b -> p (a b)")[:, c0m:512], op=ALU.mult),
                    reads=[("Pt", pb), "masks"], writes=[("Pt", pb)])

            def emit_PV(j, n):
                kv, hf = units[n]
                kbs, mi, c0m, opened = binfo(j)
                pb = (j * 8 + n) % 2

                def pvm(e):
                    ins = None
                    for ci in range(2):
                        hq = 4 * kv + 2 * ci + hf
                        for n_, kb in enumerate(kbs):
                            ins = e.matmul(o_view[:, hq, 0:65], lhsT=Pt[:, pb, kb * 256 + ci * 128:kb * 256 + (ci + 1) * 128],
                                           rhs=vaug[:, j + kb, kv, :],
                                           start=(False if opened else n_ == 0),
                                           stop=(False if opened else n_ == len(kbs) - 1))
                    return ins
                P.add("pe", pvm, reads=[("Pt", pb), "vaug_ones"] + [("vaug", j + kb) for kb in kbs],
                      writes=[("ps", 4 + kv)])

            def emit_tail_dve(j):
                P.add("dve", lambda e: e.tensor_tensor(out=den[:], in0=o_view[:, :, 64], in1=esink[:], op=ALU.add),
                      reads=orr + ["esink"], writes=["den"])
                P.add("dve", lambda e: e.reciprocal(out=rden[:], in_=den[:]), reads=["den"], writes=["rden"])
                P.add("dve", lambda e: e.tensor_tensor(out=attn_tok[:].rearrange("p (h d) -> p h d", h=16), in0=o_view[:, :, 0:64],
                                                       in1=rden[:].unsqueeze(2).broadcast_to([128, 16, 64]), op=ALU.mult),
                      reads=orr + ["rden"], writes=["attn_tok"])

            def emit_tail_act(j):
                P.add("act", lambda e: e.activation(out=xn_tok[:, 0, 0:1024], in_=attn_tok[:], func=AF.Square, accum_out=st_ssa[:, 0:1]),
                      reads=["attn_tok"], writes=[("xn_tok", 0), "st_ssa"])
                P.add("act", lambda e: e.activation(out=st_lna[:, 0:1], in_=st_ssa[:, 0:1], func=AF.Ln, scale=1.0 / 1024, bias=EPS),
                      reads=["st_ssa"], writes=["st_lna"])
                P.add("act", lambda e: e.activation(out=st_ra[:, 0:1], in_=st_lna[:, 0:1], func=AF.Exp, scale=-0.5),
                      reads=["st_lna"], writes=["st_ra"])
                P.add("act", lambda e: e.activation(out=xn_tok[:, 0, 0:1024], in_=attn_tok[:], func=AF.Copy, scale=st_ra[:, 0:1]),
                      reads=["attn_tok", "st_ra"], writes=[("xn_tok", 0)])

            def emit_tail_pe(j):
                tpa = ps_bf(3, 1)

                def tra(e):
                    ins = None
                    for c in range(8):
                        ins = e.transpose(out=tpa[:, c * 128:(c + 1) * 128], in_=xn_tok[:, 0, c * 128:(c + 1) * 128], identity=ident_bf[:])
                    return ins
                P.add("pe", tra, reads=[("xn_tok", 0), "ident_bf"], writes=[("ps", 3)])
                P.add("dve", lambda e: e.tensor_tensor(
                    out=xnT[:, 0:8, j * 128:(j + 1) * 128], in0=tpa.rearrange("p (k t) -> p k t", k=8),
                    in1=gaT[:, 0, :].unsqueeze(2).broadcast_to([128, 8, 128]), op=ALU.mult),
                    reads=[("ps", 3), "gaT"], writes=[("xnT", j)])

            def emit_samples():
                ns = 16 if "samples" in OPT else 0
                tpk = ps_bf(3, 1)

                def s_loadK(i):
                    bf = i % 2
                    o1 = dma("pool", Kd[:, bf, :, 0, :], ck[i].rearrange("j (k d) -> j k d", k=4), "ck%d" % bf, writes=[("Kd", bf)])
                    o2 = dma("pool", Kd[:, bf, :, 1, :], ck[i].rearrange("j (k d) -> j k d", k=4), "ck%d" % bf, writes=[("Kd", bf)])
                    P.group([o1, o2])

                def s_loadV(i):
                    dma("pool", vaugS[:, i % 3, :, 0:64], cv[i].rearrange("j (k d) -> j k d", k=4), "cv%d" % (i % 3), writes=[("vaugS", i % 3)])

                def s_trk(i):
                    bf = i % 2

                    def trk(e):
                        ins = None
                        for kv in range(4):
                            ins = e.transpose(out=tpk[:, kv * 128:(kv + 1) * 128],
                                              in_=Kd[:, bf, kv].rearrange("p a b -> p (a b)"), identity=ident_bf[:])
                        return ins
                    P.add("pe", trk, reads=[("Kd", bf), "ident_bf"], writes=[("ps", 3)])
                    P.add("dve", lambda e: e.tensor_copy(out=KTs[:, bf].rearrange("p a b -> p (a b)"), in_=tpk[:, 0:512]),
                          reads=[("ps", 3)], writes=[("KTs", bf)])

                def s_score(i):
                    bf = i % 2
                    for hf in range(2):
                        sbk = (2 * i + hf) % 3

                        def ssm(e, hf=hf, sbk=sbk):
                            ins = None
                            for kv in range(4):
                                ins = e.matmul(ps[:, sbk, 2 * kv:2 * kv + 2].rearrange("p (a b) -> p a b", a=2),
                                               lhsT=KTs[hf * 64:(hf + 1) * 64, bf, kv, :],
                                               rhs=qT[hf * 64:(hf + 1) * 64, 2 * kv:2 * kv + 2, i:i + 1],
                                               start=True, stop=True)
                            return ins
                        P.add("pe", ssm, reads=[("KTs", bf)] + [("qT", c, 0) for c in range(8)], writes=[("ps", sbk)])
                    P.add("dve", lambda e: e.memset(Pz[:, bf], 0.0), writes=[("Pz", bf)])
                    for hf in range(2):
                        sbk = (2 * i + hf) % 3
                        P.add("act", lambda e, hf=hf, sbk=sbk: e.activation(
                            out=Pz[:, bf].rearrange("p (kv ci hf) t -> p hf kv ci t", kv=4, ci=2, hf=2)[:, hf, :, :, i],
                            in_=ps[:, sbk, 0:8].rearrange("p (kv ci) -> p kv ci", kv=4, ci=2), func=AF.Exp),
                            reads=[("ps", sbk), ("Pz", bf)], writes=[("Pz", bf)])

                def s_pv(i):
                    bf = i % 2

                    def spv(e):
                        ins = None
                        for hq in range(16):
                            ins = e.matmul(o_view[0:16, hq, 0:65], lhsT=Pz[:, bf, hq, :], rhs=vaugS[:, i % 3, hq // 4, :],
                                           start=False, stop=False)
                        return ins
                    P.add("pe", spv, reads=[("Pz", bf), ("vaugS", i % 3), "vaugS_ones"], writes=orr)
                if ns:
                    s_loadK(0)
                    s_loadV(0)
                    s_loadK(1)
                    s_trk(0)
                for i in range(ns):
                    if i + 2 < ns:
                        s_loadK(i + 2)
                    if i + 1 < ns:
                        s_loadV(i + 1)
                        s_trk(i + 1)
                    s_score(i)
                    if i >= 1:
                        s_pv(i - 1)
                if ns:
                    s_pv(ns - 1)

            if "attn" in OPT:
                for n in range(3):
                    emit_S(0, n)
                emit_expmask(0, 0)
                emit_expmask(0, 1)
                for j, b in enumerate(blks):
                    if b == 0:
                        def opn(e):
                            ins = None
                            for i in range(4):
                                ins = e.matmul(ps[:, 4 + i, :], lhsT=zeros_bf[:], rhs=masks[:, 0].rearrange("p a b -> p (a b)"), start=True, stop=False)
                            return ins
                        P.add("pe", opn, reads=["zeros_bf", "masks"], writes=orr)
                    for n in range(8):
                        emit_PV(j, n)
                        if n + 3 < 8:
                            emit_S(j, n + 3)
                        if n + 2 < 8:
                            emit_expmask(j, n + 2)
                        if n == 1 and j > 0:
                            emit_tail_pe(j - 1)
                    if b == 0:
                        emit_samples()

                        def cls(e):
                            ins = None
                            for i in range(4):
                                ins = e.matmul(ps[:, 4 + i, :], lhsT=zeros_bf[:], rhs=masks[:, 0].rearrange("p a b -> p (a b)"), start=False, stop=True)
                            return ins
                        P.add("pe", cls, reads=["zeros_bf", "masks"], writes=orr)
                    emit_tail_dve(j)
                    if j + 1 < nb:
                        for n in range(3):
                            emit_S(j + 1, n)
                        emit_expmask(j + 1, 0)
                        emit_expmask(j + 1, 1)
                    emit_tail_act(j)
                emit_tail_pe(nb - 1)

            dma("sp", gfin_b, gfin.partition_broadcast(128).rearrange("p a b -> p (a b)"), "gfin",
                writes=[("qT", c, s_) for c in range(8) for s_ in range(2)])

            for gi in range(4 if "outproj" in OPT else 0):
                wb = wd_load(wout, gi * G)
                for j in range(nb):
                    sj = 0 if j < 3 else 1
                    for hf in range(2):
                        bk = 4 + 2 * hf

                        def omm(e, j=j, hf=hf, bk=bk, gi=gi, wb=wb):
                            ins = None
                            for dt in range(2):
                                col = (2 * hf + dt) * 512
                                for c in range(G):
                                    if gi < 2:
                                        l = xnT[:, gi * 4 + c, j * 128:(j + 1) * 128]
                                    else:
                                        l = ygT[:, (gi - 2) * 4 + c, j * 128:(j + 1) * 128]
                                    ins = e.matmul(ps[:, bk + dt, :], lhsT=l, rhs=wd[:, wb, c, col:col + 512],
                                                   start=(c == 0), stop=(c == G - 1))
                            return ins
                        rd = [("xnT", j)] if gi < 2 else [("hy", (gi - 2) * 4 + c, sj) for c in range(G)]
                        P.add("pe", omm, reads=rd + [("wd", wb)], writes=[("ps", bk), ("ps", bk + 1)])
                        if gi < 2:
                            P.add("dve", lambda e, j=j, hf=hf, bk=bk: e.tensor_tensor(
                                out=x[:, j, hf * 1024:(hf + 1) * 1024], in0=ps[:, bk:bk + 2, :].rearrange("p a b -> p (a b)"),
                                in1=x[:, j, hf * 1024:(hf + 1) * 1024], op=ALU.add),
                                reads=[("ps", bk), ("ps", bk + 1), ("x", j)], writes=[("x", j)])
                        else:
                            P.add("dve", lambda e, j=j, hf=hf, bk=bk: e.scalar_tensor_tensor(
                                out=x[:, j, hf * 1024:(hf + 1) * 1024], in0=ps[:, bk:bk + 2, :].rearrange("p a b -> p (a b)"),
                                scalar=rc[:, j:j + 1], in1=x[:, j, hf * 1024:(hf + 1) * 1024], op0=ALU.mult, op1=ALU.add),
                                reads=[("ps", bk), ("ps", bk + 1), ("x", j), ("rc", j)], writes=[("x", j)])

            return early

        def final_out(blks):
            nb = len(blks)
            for j in range(nb):
                P.add("act", lambda e, j=j: e.activation(out=xn_tok[:, j % 2, :], in_=x[:, j, :], func=AF.Square,
                                                         accum_out=st_ss[:, j:j + 1]),
                      reads=[("x", j)], writes=[("xn_tok", j % 2), ("ss", j)])
            P.add("act", lambda e: e.activation(out=st_ln[:, 0:nb], in_=st_ss[:, 0:nb], func=AF.Ln, scale=1.0 / D, bias=EPS),
                  reads=[("ss", j) for j in range(nb)], writes=["st_ln"])
            P.add("act", lambda e: e.activation(out=st_r[:, 0:nb], in_=st_ln[:, 0:nb], func=AF.Exp, scale=-0.5),
                  reads=["st_ln"], writes=["st_r"])
            gq = [("qT", c, s_) for c in range(8) for s_ in range(2)]
            for j, b in enumerate(blks):
                P.add("dve", lambda e, j=j: e.scalar_tensor_tensor(out=x[:, j, :], in0=x[:, j, :], scalar=st_r[:, j:j + 1],
                                                                   in1=gfin_b, op0=ALU.mult, op1=ALU.mult),
                      reads=[("x", j), "st_r"] + gq, writes=[("x", j)])
                if b == 0:
                    dma("sp", y_s, x[0:16, j, :], "o%d" % j, reads=[("x", j)], writes=[("outd", "y", b)])
                else:
                    dma("sp", y_p[(b - 1) * 128:b * 128, :], x[:, j, :], "o%d" % j, reads=[("x", j)], writes=[("outd", "y", b)])


        def load_x_block(j, b, via_pool=True):
            if b == 0:
                o1 = dma("sp", x[0:16, j, :], xs, "x%d" % j, writes=[("x", j)])
                o2 = dma("sp", x[112:128, j, :], meta, "x%d" % j, writes=[("x", j)])
                P.group([o1, o2])
            else:
                if via_pool:
                    dma("pool", x[:, j, :], xp[(b - 1) * 128:b * 128, :], "xq%d" % j, writes=[("x", j)])
                else:
                    dma("sp", x[:, j, :], xp[(b - 1) * 128:b * 128, :], "x%d" % j,
                        reads=([("x", 2)] if j >= 3 else []), writes=[("x", j)])

        def rstd_block(j):
            P.add("act", lambda e, j=j: e.activation(out=xn_tok[:, j % 2, :], in_=x[:, j, :], func=AF.Square,
                                                     accum_out=st_ss[:, j:j + 1]),
                  reads=[("x", j)], writes=[("xn_tok", j % 2), ("ss", j)])
            P.add("act", lambda e, j=j: e.activation(out=st_ln[:, j:j + 1], in_=st_ss[:, j:j + 1], func=AF.Ln, scale=1.0 / D, bias=EPS),
                  reads=[("ss", j)], writes=[("ln", j)])
            P.add("act", lambda e, j=j: e.activation(out=st_r[:, j:j + 1], in_=st_ln[:, j:j + 1], func=AF.Exp, scale=-0.5),
                  reads=[("ln", j)], writes=[("r", j)])

        def norm_block(gi, j):
            rstd_block(j)
            if j % 2 == 0:
                P.add("act", lambda e, j=j: e.activation(out=xn_tok[:, j % 2, :], in_=x[:, j, :], func=AF.Copy,
                                                         scale=st_r[:, j:j + 1]),
                      reads=[("x", j), ("r", j)], writes=[("xn_tok", j % 2)])
            else:
                P.add("dve", lambda e, j=j: e.tensor_scalar(out=xn_tok[:, j % 2, :], in0=x[:, j, :], scalar1=st_r[:, j:j + 1],
                                                            scalar2=None, op0=ALU.mult),
                      reads=[("x", j), ("r", j)], writes=[("xn_tok", j % 2)])
            b0 = 4 + 2 * (j % 2)
            tpv = ps_bf(b0, 2)

            def tr(e, j=j, tpv=tpv):
                ins = None
                for k in range(KC):
                    ins = e.transpose(out=tpv[:, k * 128:(k + 1) * 128], in_=xn_tok[:, j % 2, k * 128:(k + 1) * 128],
                                      identity=ident_bf[:])
                return ins
            P.add("pe", tr, reads=[("xn_tok", j % 2), "ident_bf"], writes=[("ps", b0), ("ps", b0 + 1)])
            P.add("dve", lambda e, j=j, tpv=tpv: e.tensor_tensor(
                out=xnT[:, :, j * 128:(j + 1) * 128], in0=tpv.rearrange("p (k t) -> p k t", k=KC),
                in1=gT[:, gi, :].unsqueeze(2).broadcast_to([128, KC, 128]), op=ALU.mult),
                reads=[("ps", b0), ("ps", b0 + 1), "gT"], writes=[("xnT", j)])

        def final_block(j, b):
            gq = [("qT", c, s_) for c in range(8) for s_ in range(2)]
            rstd_block(j)
            P.add("dve", lambda e, j=j: e.scalar_tensor_tensor(out=x[:, j, :], in0=x[:, j, :], scalar=st_r[:, j:j + 1],
                                                               in1=gfin_b, op0=ALU.mult, op1=ALU.mult),
                  reads=[("x", j), ("r", j)] + gq, writes=[("x", j)])
            if b == 0:
                dma("sp", y_s, x[0:16, j, :], "o%d" % j, reads=[("x", j)], writes=[("outd", "y", b)])
            else:
                dma("sp", y_p[(b - 1) * 128:b * 128, :], x[:, j, :], "o%d" % j, reads=[("x", j)], writes=[("outd", "y", b)])

        def boundary(prev_blks, next_blks, finals_done=False):
            npv = len(prev_blks or [])
            nnx = len(next_blks or [])
            ends = []
            if not finals_done:
                for j in range(npv):
                    final_block(j, prev_blks[j])
            for jn in range(nnx):
                load_x_block(jn, next_blks[jn], via_pool=bool(npv))
                if "ffn1" in OPT:
                    norm_block(0, jn)
                ends.append(len(P.ops))
            return ends

        P.add("dve", lambda e: e.memset(x[:, 0, :], 0.0), writes=[("x", 0)])
        setup()
        def hoist(jobs, ends):
            anchors = [ends[k] for k in range(2, len(ends))]
            if jobs and anchors:
                P.move_after(jobs, anchors)

        ends0 = boundary(None, SUPERS[0])
        for si_, blks in enumerate(SUPERS):
            nb = len(blks)
            if "ffn1" in OPT:
                hoist(ffn(w1g, w1u, w1d, nb), ends0)
            if si_ == 0:
                setup_late()
                if "d2d" in OPT:
                    dma("sp", ks_o[:, 0:127, :], ck[:, 1:128, :], "okv2", writes=[("outd", "ks2")])
                    dma("sp", vs_o[:, 0:127, :], cv[:, 1:128, :], "okv2", writes=[("outd", "vs2")])
                    dma("sp", cs_o[:, 0, :], sc[:, 1, :], "okv2", writes=[("outd", "cs2")])
            if dbg and si_ == 0:
                dbg_dump("d_x1", x[:, 0:2, :], [("x", 0), ("x", 1)])
            if "mixer" in OPT:
                ends1 = norm_stage(1, nb)
                hoist(mixer(si_, blks), ends1)
            else:
                dma("sp", gfin_b, gfin.partition_broadcast(128).rearrange("p a b -> p (a b)"), "gfin",
                    writes=[("qT", c, s_) for c in range(8) for s_ in range(2)])
            if dbg and si_ == 0:
                dbg_dump("d_x2", x[:, 0:2, :], [("x", 0), ("x", 1)])
            if "ffn2" in OPT:
                ends2 = norm_stage(2, nb)
                hoist(ffn(w2g, w2u, w2d, nb, after_block=lambda j, blks=blks: final_block(j, blks[j])), ends2)
            ends0 = boundary(blks, SUPERS[si_ + 1] if si_ + 1 < len(SUPERS) else None, finals_done=("ffn2" in OPT))


        all_out = sorted({t for op in P.ops for t in op.writes if isinstance(t, tuple) and t and t[0] in ("outd", "dbgout")}, key=str)
        P.add("sp", lambda e: e.nop(), reads=all_out, name="final")

        sems = {}

        def semh(key):
            if key not in sems:
                nm = ("s_%s_%s" % key).replace(" ", "")
                sems[key] = es.enter_context(nc.semaphore(nm))
            return sems[key]
        for op in P.ops:
            if op.kind == "dma":
                semh(("d", op.sem))
        for eng in ("pe", "act", "dve", "pool", "sp"):
            semh(("e", eng))
        with nc.Block() as block:
            P.emit(nc, block, semh)
    return nc


_NC_CACHE = {}


def _inputs_for_core(c, a):
    f = np.ascontiguousarray
    return {
        "xp": f(a["x_prompt"][c]),
        "xs": f(a["x_sample"][16 * c:16 * c + 16, 0, :]),
        "ck": f(a["cache_swa_k"][0, 16 * c:16 * c + 16].reshape(16, 128, 256)),
        "cv": f(a["cache_swa_v"][0, 16 * c:16 * c + 16].reshape(16, 128, 256)),
        "sc": f(a["state_conv"][0, 16 * c:16 * c + 16]),
        "meta": a["meta_tokens"],
        "gx": a["_gx"], "gac": a["_gac"], "wcv": a["_wcv"], "sinks": a["_sinks"], "gfin": a["_gfin"],
        "w1g": f(a["w1_gate"][0][:, :NGRP * 512]), "w1u": f(a["w1_up"][0][:, :NGRP * 512]), "w1d": f(a["w1_down"][0][:NGRP * 512]),
        "w2g": f(a["w2_gate"][0][:, :NGRP * 512]), "w2u": f(a["w2_up"][0][:, :NGRP * 512]), "w2d": f(a["w2_down"][0][:NGRP * 512]),
        "win": a["w_in"][0], "wout": a["w_out"][0],
    }


def kernel(**inputs):
    a = {k: np.asarray(v) for k, v in inputs.items()}
    f = np.ascontiguousarray
    a["_gx"] = f(np.stack([a["g_ffn1"][0], a["g_mix"][0], a["g_ffn2"][0]]).reshape(3, 16, 128))
    a["_gac"] = f(np.stack([a["g_attn_out"][0], a["g_conv_out"][0]]).reshape(2, 8, 128))
    a["_wcv"] = f(a["w_conv"][0].reshape(24, 128))
    a["_sinks"] = f(a["attn_sinks"].reshape(1, 16))
    a["_gfin"] = f(a["g_final"].reshape(1, D))
    if "nc" not in _NC_CACHE:
        _NC_CACHE["nc"] = build_program()
    nc = _NC_CACHE["nc"]
    in_maps = [_inputs_for_core(c, a) for c in range(NCORES)]
    res = run_bass_kernel_spmd(nc, in_maps, core_ids=list(range(NCORES)))
    R = res.results
    y_prompt = np.stack([R[c]["y_p"] for c in range(NCORES)]).astype(np.float32)
    y_sample = np.concatenate([R[c]["y_s"] for c in range(NCORES)]).reshape(128, 1, D).astype(np.float32)
    kp = np.stack([R[c]["kp_o"] for c in range(NCORES)]).reshape(1, 8, 128, 4, 64).astype(np.float32)
    vp = np.stack([R[c]["vp_o"] for c in range(NCORES)]).reshape(1, 8, 128, 4, 64).astype(np.float32)
    cp = np.stack([R[c]["cp_o"] for c in range(NCORES)]).reshape(1, 8, 2, 1024).astype(np.float32)
    ks = np.concatenate([R[c]["ks_o"] for c in range(NCORES)]).reshape(1, 128, 128, 4, 64).astype(np.float32)
    vs = np.concatenate([R[c]["vs_o"] for c in range(NCORES)]).reshape(1, 128, 128, 4, 64).astype(np.float32)
    cs = np.concatenate([R[c]["cs_o"] for c in range(NCORES)]).reshape(1, 128, 2, 1024).astype(np.float32)
    return (y_prompt, y_sample, kp, vp, cp, ks, vs, cs)
```

```python
import numpy as np
import concourse.bass as bass
import concourse.mybir as mybir
from concourse.bass_utils import run_bass_kernel_spmd

F32 = mybir.dt.float32
BF16 = mybir.dt.bfloat16
AF = mybir.ActivationFunctionType
ALU = mybir.AluOpType

NCORES = 8
D = 2048
KC = 16
FF = 5632
G = 4
NGRP = FF // 128 // G
NBMAX = 6
TMAX = NBMAX * 128
SUPERS = [list(range(0, 6)), list(range(6, 12)), list(range(12, 17))]
EPS = 1e-5
NR = 3
OPT = {"ffn1", "mixer", "ffn2", "d2d", "attn", "conv", "samples", "outproj", "mq", "mk", "mv", "mkt", "mhist", "mrc"}


class Op:
    __slots__ = ("eng", "fn", "deps", "kind", "sem", "target", "signal", "count", "name", "reads", "writes", "grp")


class Prog:
    def __init__(self):
        self.ops = []
        self.last_w = {}
        self.readers = {}
        self.sem_counts = {}

    def add(self, eng, fn, reads=(), writes=(), kind="c", sem=None, name=""):
        op = Op()
        op.eng, op.fn, op.kind, op.sem, op.name = eng, fn, kind, sem, name
        op.signal, op.count, op.target = False, 0, 0
        op.reads = list(reads)
        op.writes = list(writes) + [t for t in op.reads if isinstance(t, tuple) and t[0] == "ps" and t not in writes]
        op.grp = None
        op.deps = {}
        self.ops.append(op)
        return op

    def group(self, ops):
        self.ngroups = getattr(self, "ngroups", 0) + 1
        for o in ops:
            o.grp = self.ngroups

    def finalize(self):
        last_w, readers, sem_counts, groups = {}, {}, {}, {}
        for op in self.ops:
            deps = {}
            for t in op.reads:
                w = last_w.get(t)
                if w is not None:
                    deps[w] = True
            for t in op.writes:
                w = last_w.get(t)
                if w is not None and w not in deps:
                    deps[w] = False
                for r in readers.get(t, ()):
                    if r not in deps:
                        deps[r] = False
            deps.pop(op, None)
            op.deps = deps
            for t in op.reads:
                readers.setdefault(t, []).append(op)
            for t in op.writes:
                last_w[t] = op
                readers[t] = []
            if op.kind == "dma":
                c = sem_counts.get(op.sem, 0) + 16
                sem_counts[op.sem] = c
                op.target = c
                if op.grp is not None:
                    groups.setdefault(op.grp, []).append(op)
        for ops in groups.values():
            t = max(o.target for o in ops)
            for o in ops:
                o.target = t
                for d in list(o.deps):
                    if d in ops:
                        del o.deps[d]
        self.last_w = last_w

    def move_after(self, jobs, anchors):
        ops = self.ops
        taken = [(anchors[k], ops[a:b]) for (a, b, k) in jobs if k < len(anchors)]
        drop = set()
        for _, seg in taken:
            drop.update(id(o) for o in seg)
        new = []
        for i, o in enumerate(ops):
            for pos, seg in taken:
                if pos == i:
                    new.extend(seg)
            if id(o) not in drop:
                new.append(o)
        self.ops = new

    @staticmethod
    def _needs_wait(op, d, raw):
        if d.kind == "dma":
            return True
        if d.eng != op.eng:
            return True
        if op.kind == "dma":
            return True
        if op.eng == "pe":
            return False
        return True

    def emit(self, nc, block, semh):
        self.finalize()
        for op in self.ops:
            for d, raw in op.deps.items():
                if d.kind == "c" and self._needs_wait(op, d, raw):
                    d.signal = True
        cnt = {}
        for op in self.ops:
            if op.kind == "c" and op.signal:
                cnt[op.eng] = cnt.get(op.eng, 0) + 1
                op.count = cnt[op.eng]
        by_eng = {}
        for op in self.ops:
            by_eng.setdefault(op.eng, []).append(op)

        def run(e, ops):
            waited = {}
            for op in ops:
                need = {}
                for d, raw in op.deps.items():
                    if not self._needs_wait(op, d, raw):
                        continue
                    if d.kind == "dma":
                        key, val = ("d", d.sem), d.target
                    else:
                        key, val = ("e", d.eng), d.count
                    if waited.get(key, 0) < val and need.get(key, 0) < val:
                        need[key] = val
                for key, val in need.items():
                    e.wait_ge(semh(key), val)
                    waited[key] = val
                ins = op.fn(e)
                if op.kind == "dma":
                    ins.then_inc(semh(("d", op.sem)), 16)
                elif op.signal:
                    ins.then_inc(semh(("e", op.eng)), 1)

        names = {"pe": "tensor", "act": "scalar", "dve": "vector", "pool": "gpsimd", "sp": "sync"}
        for eng, ops in by_eng.items():
            getattr(block, names[eng])(lambda e, ops=ops: run(e, ops))


def build_program(dbg=None):
    nc = bass.Bass("TRN2", target_bir_lowering=False)
    P = Prog()

    def din(name, shape):
        return nc.dram_tensor(name, list(shape), F32, kind="ExternalInput").ap()

    def dout(name, shape):
        return nc.dram_tensor(name, list(shape), F32, kind="ExternalOutput").ap()

    xp = din("xp", [2048, D])
    xs = din("xs", [16, D])
    ck = din("ck", [16, 128, 256])
    cv = din("cv", [16, 128, 256])
    sc = din("sc", [16, 2, 1024])
    meta = din("meta", [16, D])
    gx = din("gx", [3, 16, 128])
    gac = din("gac", [2, 8, 128])
    wcv = din("wcv", [24, 128])
    sinks = din("sinks", [1, 16])
    gfin = din("gfin", [1, D])
    FFW = NGRP * G * 128
    w1g = din("w1g", [D, FFW]); w1u = din("w1u", [D, FFW]); w1d = din("w1d", [FFW, D])
    w2g = din("w2g", [D, FFW]); w2u = din("w2u", [D, FFW]); w2d = din("w2d", [FFW, D])
    win = din("win", [D, 4608]); wout = din("wout", [D, D])

    y_p = dout("y_p", [2048, D]); y_s = dout("y_s", [16, D])
    kp_o = dout("kp_o", [128, 256]); vp_o = dout("vp_o", [128, 256]); cp_o = dout("cp_o", [2, 1024])
    ks_o = dout("ks_o", [16, 128, 256]); vs_o = dout("vs_o", [16, 128, 256]); cs_o = dout("cs_o", [16, 2, 1024])
    dbg_aps = {}
    if dbg:
        for nm, shp in dbg.items():
            dbg_aps[nm] = dout(nm, shp)

    colview = lambda W: W.rearrange("(k p) n -> p k n", p=128)
    rowview = lambda W: W.rearrange("(c p) n -> p c n", p=128)

    from contextlib import ExitStack
    es = ExitStack()
    with es:
        def sb(name, shape, dt):
            return es.enter_context(nc.sbuf_tensor(name, list(shape), dt))

        x = sb("x", [128, NBMAX, D], F32)
        xnT = sb("xnT", [128, KC, TMAX], BF16)
        hy = sb("hy", [128, 8, TMAX], BF16)
        wd = sb("wd", [128, 2, G, D], BF16)
        ring = sb("ring", [128, NR, KC, 256], BF16)
        xn_tok = sb("xn_tok", [128, 2, D], BF16)
        sgc = sb("sgc", [128, 4, 384], F32)
        csb = sgc[:].rearrange("p a b -> p (a b)")
        qT = sb("qT", [128, 8, TMAX], BF16)
        kT = sb("kT", [128, 4, 128 + TMAX], BF16)
        vaug = sb("vaug", [128, NBMAX + 1, 4, 65], BF16)
        u = sb("u", [128, TMAX + 2], F32)
        acc = sb("acc", [128, TMAX], F32)
        ysq = sb("ysq", [128, TMAX], F32)
        uhist = sb("uhist", [128, 8, 2], F32)
        histT = sb("histT", [128, 2, 8, 16], F32)
        Pt = sb("Pt", [128, 2, 512], BF16)
        attn_tok = sb("attn_tok", [128, 1024], F32)
        kv_tok = sb("kv_tok", [128, 2, 256], F32)
        Kd = sb("Kd", [128, 2, 4, 2, 64], BF16)
        KTs = sb("KTs", [128, 2, 4, 128], BF16)
        vaugS = sb("vaugS", [128, 3, 4, 65], BF16)
        Pz = sb("Pz", [128, 2, 16, 16], BF16)
        ones_bf = sb("ones_bf", [128, 128], BF16)
        ident_bf = sb("ident_bf", [128, 128], BF16)
        ones_f = sb("ones_f", [128, 128], F32)
        ident_f = sb("ident_f", [128, 128], F32)
        zeros_bf = sb("zeros_bf", [128, 128], BF16)
        mbase = sb("mbase", [128, 4, 128], BF16)
        masks = sb("masks", [128, 3, 4, 128], BF16)
        gT = sb("gT", [128, 3, 16], F32)
        gaT = sb("gaT", [128, 2, 8], F32)
        wcT = sb("wcT", [128, 24], F32)
        esink = sb("esink", [128, 16], F32)
        st_ss = sb("st_ss", [128, 8], F32)
        st_ln = sb("st_ln", [128, 8], F32)
        st_r = sb("st_r", [128, 8], F32)
        st_ssa = sb("st_ssa", [128, 2], F32)
        st_lna = sb("st_lna", [128, 2], F32)
        st_ra = sb("st_ra", [128, 2], F32)
        den = sb("den", [128, 16], F32)
        rden = sb("rden", [128, 16], F32)
        rc = sb("rc", [128, 8], F32)
        ps = es.enter_context(nc.psum_tensor("ps", [128, 8, 512], F32))

        hT = hy[:].rearrange("p (u c) t -> p u c t", u=2)
        ygT = hy
        gfin_b = qT[:].rearrange("p a b -> p (a b)")[:, 0:2 * D].bitcast(F32)
        o_view = ps[:, 4:8, :].rearrange("p b (h w) -> p (b h) w", w=128)

        def ps_bf(bank0, nbanks):
            v = ps[:, bank0:bank0 + nbanks, :].bitcast(BF16)
            return v.rearrange("p a b -> p (a b)")

        cnt = {"ring": 0, "wd": 0, "zb": 0, "gu": 0}

        def dma(eng, out, in_, sem, reads=(), writes=(), name="", **kw):
            return P.add(eng, lambda e, out=out, in_=in_, kw=kw: e.dma_start(out=out, in_=in_, **kw),
                         reads=reads, writes=writes, kind="dma", sem=sem, name=name)

        def dbg_dump(nm, src, reads):
            if nm in dbg_aps:
                dma("sp", dbg_aps[nm], src, "dbg_" + nm, reads=reads, writes=[("dbgout", nm)])

        def ring_load(W, pieces):
            s = cnt["ring"] % NR
            cnt["ring"] += 1
            ops = []
            cv_ = colview(W)
            for (d0, w, s0) in pieces:
                ops.append(dma("pool", ring[:, s, :, d0:d0 + w], cv_[:, :, s0:s0 + w], "ring%d" % s,
                               writes=[("ring", s)]))
            P.group(ops)
            return s

        def wd_load(W, c0):
            b = cnt["wd"] % 2
            cnt["wd"] += 1
            dma("pool", wd[:, b], rowview(W)[:, c0:c0 + G, :], "wd%d" % b, writes=[("wd", b)])
            return b

        def subs_of(nb):
            return [(0, 384, 0, 3), (384, nb * 128, 3, nb)]

        def setup():
            P.add("pool", lambda e: e.memset(ones_bf[:], 1.0), writes=["ones_bf"])
            P.add("pool", lambda e: e.memset(ones_f[:], 1.0), writes=["ones_f"])
            P.add("pool", lambda e: e.memset(zeros_bf[:], 0.0), writes=["zeros_bf"])
            P.add("pool", lambda e: e.affine_select(out=ident_bf[:], in_=ones_bf[:], pattern=[[-1, 128]],
                                                    compare_op=ALU.is_equal, fill=0.0, base=0, channel_multiplier=1),
                  reads=["ones_bf"], writes=["ident_bf"])
            P.add("pool", lambda e: e.affine_select(out=ident_f[:], in_=ones_f[:], pattern=[[-1, 128]],
                                                    compare_op=ALU.is_equal, fill=0.0, base=0, channel_multiplier=1),
                  reads=["ones_f"], writes=["ident_f"])
            stg = attn_tok[:].rearrange("p (a b) -> p a b", b=128)
            ops = []
            for i in range(3):
                ops.append(dma("sp", stg[0:16, i, :], gx[i], "cst", writes=["attn_tok"]))
            for i in range(2):
                ops.append(dma("sp", stg[0:8, 3 + i, :], gac[i], "cst", writes=["attn_tok"]))
            ops.append(dma("sp", stg[0:24, 5, :], wcv, "cst", writes=["attn_tok"]))
            ops.append(dma("sp", esink[:], sinks.partition_broadcast(128).rearrange("p a b -> p (a b)"), "cst", writes=["esink"]))
            P.group(ops)
            P.add("act", lambda e: e.activation(out=esink[:], in_=esink[:], func=AF.Exp), reads=["esink"], writes=["esink"])

            def tr(e):
                ins = None
                for i in range(3):
                    ins = e.transpose(out=ps[:, 3, i * 16:(i + 1) * 16], in_=stg[0:16, i, :], identity=ident_f[0:16, 0:16])
                for i in range(2):
                    ins = e.transpose(out=ps[:, 3, 48 + i * 8:48 + (i + 1) * 8], in_=stg[0:8, 3 + i, :], identity=ident_f[0:8, 0:8])
                ins = e.transpose(out=ps[:, 3, 64:88], in_=stg[0:24, 5, :], identity=ident_f[0:24, 0:24])
                return ins
            P.add("pe", tr, reads=["attn_tok", "ident_f"], writes=[("ps", 3)])
            P.add("dve", lambda e: e.tensor_copy(out=gT[:].rearrange("p a b -> p (a b)"), in_=ps[:, 3, 0:48]),
                  reads=[("ps", 3)], writes=["gT"])
            P.add("dve", lambda e: e.tensor_copy(out=gaT[:].rearrange("p a b -> p (a b)"), in_=ps[:, 3, 48:64]),
                  reads=[("ps", 3)], writes=["gaT"])
            P.add("dve", lambda e: e.tensor_copy(out=wcT[:], in_=ps[:, 3, 64:88]),
                  reads=[("ps", 3)], writes=["wcT", "attn_tok"])

        def setup_late():
            P.add("pool", lambda e: e.affine_select(out=mbase[:, 0, :], in_=ones_bf[:], pattern=[[-1, 128]],
                                                    compare_op=ALU.is_ge, fill=0.0, base=0, channel_multiplier=1),
                  reads=["ones_bf"], writes=["mb0"])
            P.add("pool", lambda e: e.affine_select(out=mbase[:, 1, :], in_=ones_bf[:], pattern=[[1, 128]],
                                                    compare_op=ALU.is_ge, fill=0.0, base=0, channel_multiplier=-1),
                  reads=["ones_bf"], writes=["mb1"])
            P.add("pool", lambda e: e.affine_select(out=mbase[:, 2, :], in_=mbase[:, 0, :], pattern=[[0, 128]],
                                                    compare_op=ALU.is_ge, fill=0.0, base=-112, channel_multiplier=1),
                  reads=["mb0"], writes=["mb2"])
            P.add("pool", lambda e: e.affine_select(out=mbase[:, 3, :], in_=mbase[:, 1, :], pattern=[[0, 128]],
                                                    compare_op=ALU.is_ge, fill=0.0, base=-112, channel_multiplier=1),
                  reads=["mb1"], writes=["mb3"])
            P.add("pool", lambda e: e.tensor_copy(out=mbase[:, 3, 0:16], in_=ident_bf[:, 0:16]),
                  reads=["ident_bf", "mb3"], writes=["mb3"])
            plan = [(0, 0, 0), (0, 1, 0), (0, 2, 1), (0, 3, 1),
                    (1, 0, 2), (1, 1, 2), (1, 2, 1), (1, 3, 1),
                    (2, 0, 0), (2, 1, 0), (2, 2, 3), (2, 3, 3)]
            for (m, q, src) in plan:
                P.add("pool", lambda e, m=m, q=q, src=src: e.tensor_copy(out=masks[:, m, q, :], in_=mbase[:, src, :]),
                      reads=["mb%d" % src], writes=["masks"])
            P.add("pool", lambda e: e.memset(vaug[:, :, :, 64:65], 1.0), writes=["vaug_ones"])
            P.add("pool", lambda e: e.memset(vaugS[:, :, :, 64:65], 1.0), writes=["vaugS_ones"])
            P.add("pool", lambda e: e.memset(uhist[:], 0.0), writes=["uhist"])
            P.add("pool", lambda e: e.memset(u[:, 0:2], 0.0), writes=["u_hist_cols"])

        def load_x(blks):
            for j, b in enumerate(blks):
                if b == 0:
                    P.add("dve", lambda e, j=j: e.memset(x[:, j, :], 0.0), writes=[("x", j)])
                    o1 = dma("sp", x[0:16, j, :], xs, "x%d" % j, writes=[("x", j)])
                    o2 = dma("sp", x[112:128, j, :], meta, "x%d" % j, writes=[("x", j)])
                    P.group([o1, o2])
                else:
                    dma("sp", x[:, j, :], xp[(b - 1) * 128:b * 128, :], "x%d" % j,
                        reads=([("x", 2)] if j >= 3 else []), writes=[("x", j)])

        def norm_stage(gi, nb):
            ends = []
            for j in range(nb):
                norm_block(gi, j)
                ends.append(len(P.ops))
            return ends

        def ffn(Wg, Wu, Wd_, nb, after_block=None):
            subs = subs_of(nb)
            early = []
            for g in range(NGRP):
                a_l = len(P.ops)
                if g > 0:
                    wb = wd_load(Wd_, g * G)
                hb = g % 2
                for pair in range(2):
                    c0 = g * 512 + pair * 256
                    sg_ = ring_load(Wg, [(0, 256, c0)])
                    su_ = ring_load(Wu, [(0, 256, c0)])
                    if g == 0 and pair == 0:
                        wb = wd_load(Wd_, 0)
                        early.append((a_l, len(P.ops), 0))

                    def mmchain(e, slot, bank, half, t0, t1):
                        ins = None
                        for k in range(KC):
                            ins = e.matmul(ps[:, bank, 0:t1 - t0], lhsT=ring[:, slot, k, half * 128:(half + 1) * 128],
                                           rhs=xnT[:, k, t0:t1], start=(k == 0), stop=(k == KC - 1))
                        return ins
                    units = [(half, si) for si in range(2) for half in range(2)]

                    def job_g(ui):
                        half, si = units[ui]
                        t0, t1, j0, j1 = subs[si]
                        w = t1 - t0
                        par = cnt["gu"] % 2
                        cnt["gu"] += 1
                        xr = [("xnT", j) for j in range(j0, j1)]
                        P.add("pe", lambda e, s=sg_, b=par, half=half, t0=t0, t1=t1: mmchain(e, s, b, half, t0, t1),
                              reads=[("ring", sg_)] + xr, writes=[("ps", par)])
                        P.add("act", lambda e, par=par, w=w, ui=ui: e.activation(
                            out=sgc[:, ui, 0:w], in_=ps[:, par, 0:w], func=AF.Silu),
                            reads=[("ps", par)], writes=[("sg", ui)])

                    def job_u(ui):
                        half, si = units[ui]
                        t0, t1, j0, j1 = subs[si]
                        w = t1 - t0
                        fcl = pair * 2 + half
                        par = cnt["gu"] % 2
                        cnt["gu"] += 1
                        xr = [("xnT", j) for j in range(j0, j1)]
                        P.add("pe", lambda e, s=su_, b=2 + par, half=half, t0=t0, t1=t1: mmchain(e, s, b, half, t0, t1),
                              reads=[("ring", su_)] + xr, writes=[("ps", 2 + par)])
                        P.add("dve", lambda e, par=par, w=w, ui=ui, hb=hb, fcl=fcl, t0=t0, t1=t1: e.tensor_tensor(
                            out=hT[:, hb, fcl, t0:t1], in0=sgc[:, ui, 0:w], in1=ps[:, 2 + par, 0:w],
                            op=ALU.mult),
                            reads=[("sg", ui), ("ps", 2 + par)], writes=[("hy", hb * 4 + fcl, si)])
                    if g == 0 and pair == 0:
                        for kk, (fn_, ui) in enumerate(((job_g, 0), (job_g, 1), (job_u, 0), (job_u, 1))):
                            a_ = len(P.ops)
                            fn_(ui)
                            early.append((a_, a_ + 1, kk))
                        for fn_, ui in ((job_g, 2), (job_g, 3), (job_u, 2), (job_u, 3)):
                            fn_(ui)
                    else:
                        for ui in range(4):
                            job_g(ui)
                        for ui in range(4):
                            job_u(ui)
                for j in range(nb):
                    si = 0 if j < 3 else 1
                    for hf in range(2):
                        bk = 4 + 2 * hf

                        def dmm(e, j=j, hf=hf, bk=bk, hb=hb, wb=wb):
                            ins = None
                            for dt in range(2):
                                col = (2 * hf + dt) * 512
                                for c in range(G):
                                    ins = e.matmul(ps[:, bk + dt, :], lhsT=hT[:, hb, c, j * 128:(j + 1) * 128],
                                                   rhs=wd[:, wb, c, col:col + 512], start=(c == 0), stop=(c == G - 1))
                            return ins
                        P.add("pe", dmm, reads=[("hy", hb * 4 + c, si) for c in range(G)] + [("wd", wb)],
                              writes=[("ps", bk), ("ps", bk + 1)])
                        P.add("dve", lambda e, j=j, hf=hf, bk=bk: e.scalar_tensor_tensor(
                            out=x[:, j, hf * 1024:(hf + 1) * 1024],
                            in0=ps[:, bk:bk + 2, :].rearrange("p a b -> p (a b)"), scalar=0.5,
                            in1=x[:, j, hf * 1024:(hf + 1) * 1024], op0=ALU.mult, op1=ALU.add),
                            reads=[("ps", bk), ("ps", bk + 1), ("x", j)], writes=[("x", j)])
                    if g == NGRP - 1 and after_block is not None:
                        after_block(j)

            return early

        def mixer(si_, blks):
            nb = len(blks)
            T = nb * 128
            subs = subs_of(nb)
            first = (si_ == 0)
            has0 = 0 in blks
            has16 = 16 in blks
            ststg = attn_tok

            def zbank():
                b = cnt["zb"] % 4
                cnt["zb"] += 1
                return b

            def zchain(slot, half, t0, t1, bank, xr):
                w = t1 - t0

                def f(e):
                    ins = None
                    for k in range(KC):
                        ins = e.matmul(ps[:, bank, 0:w], lhsT=ring[:, slot, k, half * 128:(half + 1) * 128],
                                       rhs=xnT[:, k, t0:t1], start=(k == 0), stop=(k == KC - 1))
                    return ins
                P.add("pe", f, reads=[("ring", slot)] + xr, writes=[("ps", bank)])

            if not first:
                nbp = len(SUPERS[si_ - 1])
                P.add("act", lambda e: e.activation(out=kT[:, :, 0:128], in_=kT[:, :, nbp * 128:(nbp + 1) * 128], func=AF.Copy),
                      reads=[("kT", nbp)], writes=[("kT", 0)])
                P.add("act", lambda e: e.activation(out=vaug[:, 0, :, 0:64], in_=vaug[:, nbp, :, 0:64], func=AF.Copy),
                      reads=[("vaug", nbp)], writes=[("vaug", 0)])
            if has0 and "mhist" in OPT:
                for jj in range(2):
                    dma("sp", ststg[0:16, :], sc[:, jj, :], "cst2", reads=[], writes=["attn_tok"])

                    def trh(e):
                        ins = None
                        for cj in range(8):
                            ins = e.transpose(out=ps[:, 6, cj * 16:(cj + 1) * 16], in_=ststg[0:16, cj * 128:(cj + 1) * 128],
                                              identity=ident_f[0:16, 0:16])
                        return ins
                    P.add("pe", trh, reads=["attn_tok", "ident_f"], writes=[("ps", 6)])
                    P.add("dve", lambda e, jj=jj: e.tensor_copy(out=histT[:, jj].rearrange("p a b -> p (a b)"), in_=ps[:, 6, 0:128]),
                          reads=[("ps", 6)], writes=["histT", "attn_tok"])

            early = []
            for qp in range(2 if "mq" in OPT else 0):
                slots = []
                for qs in (2 * qp, 2 * qp + 1):
                    a_l = len(P.ops)
                    slots.append(ring_load(win, [(0, 256, qs * 256)]))
                    if qp == 0:
                        early.append((a_l, len(P.ops), 0 if qs == 0 else 2))
                for si, (t0, t1, j0, j1) in enumerate(subs):
                    for qi, qs in enumerate((2 * qp, 2 * qp + 1)):
                        for half in range(2):
                            c8 = qs * 2 + half
                            a_ = len(P.ops)
                            bk = zbank()
                            zchain(slots[qi], half, t0, t1, bk, [("xnT", j) for j in range(j0, j1)])
                            P.add("act", lambda e, bk=bk, c8=c8, t0=t0, t1=t1: e.activation(
                                out=qT[:, c8, t0:t1], in_=ps[:, bk, 0:t1 - t0], func=AF.Copy, scale=0.125),
                                reads=[("ps", bk)], writes=[("qT", c8, si)])
                            if qp == 0 and si == 0:
                                early.append((a_, a_ + 1, 2 * qi + half))
            for ks_ in range(2 if "mk" in OPT else 0):
                s = ring_load(win, [(i * 64, 64, 1024 + (2 * ks_ + i // 2) * 64) for i in range(4)])
                for half in range(2):
                    kv = ks_ * 2 + half
                    for si, (t0, t1, j0, j1) in enumerate(subs):
                        bk = zbank()
                        zchain(s, half, t0, t1, bk, [("xnT", j) for j in range(j0, j1)])
                        P.add("act", lambda e, bk=bk, kv=kv, t0=t0, t1=t1: e.activation(
                            out=kT[:, kv, 128 + t0:128 + t1], in_=ps[:, bk, 0:t1 - t0], func=AF.Copy),
                            reads=[("ps", bk)], writes=[("kT", j + 1) for j in range(j0, j1)])
            if "mv" in OPT:
                s = ring_load(win, [(0, 256, 1280)])
            for j, b in enumerate(blks if "mv" in OPT else []):
                vb = 6 + (j % 2)

                def vmm(e, j=j, s=s, vb=vb):
                    ins = None
                    for k in range(KC):
                        ins = e.matmul(ps[:, vb, 0:256], lhsT=xnT[:, k, j * 128:(j + 1) * 128], rhs=ring[:, s, k, :],
                                       start=(k == 0), stop=(k == KC - 1))
                    return ins
                P.add("pe", vmm, reads=[("ring", s), ("xnT", j)], writes=[("ps", vb)])
                P.add("act", lambda e, j=j, vb=vb: e.activation(out=vaug[:, j + 1, :, 0:64],
                                                                in_=ps[:, vb, 0:256].rearrange("p (a b) -> p a b", a=4), func=AF.Copy),
                      reads=[("ps", vb)], writes=[("vaug", j + 1)])
                if b == 0 or b == 16:
                    P.add("dve", lambda e, vb=vb: e.tensor_copy(out=kv_tok[:, 1, :], in_=ps[:, vb, 0:256]),
                          reads=[("ps", vb)], writes=[("kv_tok", 1)])
                    if b == 16:
                        dma("sp", vp_o, kv_tok[:, 1, :], "okv1", reads=[("kv_tok", 1)], writes=[("outd", "vp")])
                    else:
                        dma("sp", vs_o[:, 127, :], kv_tok[0:16, 1, :], "okv1", reads=[("kv_tok", 1)], writes=[("outd", "vs")])
            if (has0 or has16) and "mkt" in OPT:
                s = ring_load(win, [(0, 256, 1024)])
                for j, b in enumerate(blks):
                    if b != 0 and b != 16:
                        continue

                    def kmm(e, j=j, s=s):
                        ins = None
                        for k in range(KC):
                            ins = e.matmul(ps[:, 7, 0:256], lhsT=xnT[:, k, j * 128:(j + 1) * 128], rhs=ring[:, s, k, :],
                                           start=(k == 0), stop=(k == KC - 1))
                        return ins
                    P.add("pe", kmm, reads=[("ring", s), ("xnT", j)], writes=[("ps", 7)])
                    P.add("dve", lambda e: e.tensor_copy(out=kv_tok[:, 0, :], in_=ps[:, 7, 0:256]),
                          reads=[("ps", 7)], writes=[("kv_tok", 0)])
                    if b == 16:
                        dma("sp", kp_o, kv_tok[:, 0, :], "okv0", reads=[("kv_tok", 0)], writes=[("outd", "kp")])
                    else:
                        dma("sp", ks_o[:, 127, :], kv_tok[0:16, 0, :], "okv0", reads=[("kv_tok", 0)], writes=[("outd", "ks")])
            col_of = {"C": 2560, "h": 3584, "B": 1536}
            seq = []
            for cj in range(8):
                for nm in ("C", "h", "B"):
                    seq.append((nm, cj))
            slot_of = {}
            for i in range(0, len(seq), 2):
                pieces = [(0, 128, col_of[seq[i][0]] + seq[i][1] * 128), (128, 128, col_of[seq[i + 1][0]] + seq[i + 1][1] * 128)]
                slot_of[i // 2] = pieces
            cur_slot = None
            pending_ssq = []
            if "conv" not in OPT:
                seq = []
            for idx, (nm, cj) in enumerate(seq):
                if idx % 2 == 0:
                    cur_slot = ring_load(win, slot_of[idx // 2])
                half = idx % 2
                if nm == "C":
                    for si, (t0, t1, j0, j1) in enumerate(subs):
                        bk = zbank()
                        zchain(cur_slot, half, t0, t1, bk, [("xnT", j) for j in range(j0, j1)])
                        P.add("act", lambda e, bk=bk, t0=t0, t1=t1: e.activation(out=csb[:, t0:t1], in_=ps[:, bk, 0:t1 - t0], func=AF.Copy),
                              reads=[("ps", bk)], writes=[("sg", 2 * si), ("sg", 2 * si + 1)])
                elif nm == "h":
                    while pending_ssq:
                        pending_ssq.pop(0)()
                    if not first:
                        P.add("act", lambda e, cj=cj: e.activation(out=u[:, 0:2], in_=uhist[:, cj, :], func=AF.Copy),
                              reads=["uhist"], writes=["u_hist_cols"])
                    for si, (t0, t1, j0, j1) in enumerate(subs):
                        bk = zbank()
                        zchain(cur_slot, half, t0, t1, bk, [("xnT", j) for j in range(j0, j1)])
                        P.add("dve", lambda e, bk=bk, t0=t0, t1=t1: e.tensor_tensor(
                            out=u[:, 2 + t0:2 + t1], in0=csb[:, t0:t1], in1=ps[:, bk, 0:t1 - t0], op=ALU.mult),
                            reads=[("ps", bk), ("sg", 2 * si), ("sg", 2 * si + 1)], writes=[("u", si)])
                else:
                    bks = []
                    for si, (t0, t1, j0, j1) in enumerate(subs):
                        bk = zbank()
                        bks.append(bk)
                        zchain(cur_slot, half, t0, t1, bk, [("xnT", j) for j in range(j0, j1)])
                    ur = [("u", 0), ("u", 1), "u_hist_cols"]
                    P.add("act", lambda e, cj=cj: e.activation(out=acc[:, 0:T], in_=u[:, 2:2 + T], func=AF.Copy,
                                                               scale=wcT[:, 16 + cj:17 + cj]),
                          reads=ur + ["wcT"], writes=["acc"])
                    P.add("dve", lambda e, cj=cj: e.scalar_tensor_tensor(out=acc[:, 0:T], in0=u[:, 1:1 + T], scalar=wcT[:, 8 + cj:9 + cj],
                                                                         in1=acc[:, 0:T], op0=ALU.mult, op1=ALU.add),
                          reads=ur + ["acc"], writes=["acc"])
                    P.add("dve", lambda e, cj=cj: e.scalar_tensor_tensor(out=acc[:, 0:T], in0=u[:, 0:T], scalar=wcT[:, cj:cj + 1],
                                                                         in1=acc[:, 0:T], op0=ALU.mult, op1=ALU.add),
                          reads=ur + ["acc"], writes=["acc"])
                    if has0:
                        P.add("dve", lambda e, cj=cj: e.tensor_scalar(out=acc[:, 0:16], in0=u[:, 2:18], scalar1=wcT[:, 16 + cj:17 + cj],
                                                                      scalar2=None, op0=ALU.mult),
                              reads=ur + ["acc"], writes=["acc"])
                        P.add("dve", lambda e, cj=cj: e.scalar_tensor_tensor(out=acc[:, 0:16], in0=histT[:, 1, cj, :], scalar=wcT[:, 8 + cj:9 + cj],
                                                                             in1=acc[:, 0:16], op0=ALU.mult, op1=ALU.add),
                              reads=["histT", "acc"], writes=["acc"])
                        P.add("dve", lambda e, cj=cj: e.scalar_tensor_tensor(out=acc[:, 0:16], in0=histT[:, 0, cj, :], scalar=wcT[:, cj:cj + 1],
                                                                             in1=acc[:, 0:16], op0=ALU.mult, op1=ALU.add),
                              reads=["histT", "acc"], writes=["acc"])
                    for si, (t0, t1, j0, j1) in enumerate(subs):
                        P.add("dve", lambda e, bk=bks[si], t0=t0, t1=t1: e.tensor_tensor(
                            out=acc[:, t0:t1], in0=acc[:, t0:t1], in1=ps[:, bk, 0:t1 - t0], op=ALU.mult),
                            reads=[("ps", bks[si]), "acc"], writes=["acc"])
                    if has16:
                        P.add("pe", lambda e, cj=cj: e.transpose(out=ps[:, 6, cj * 128:(cj + 1) * 128] if cj < 4 else
                                                                 ps[:, 7, (cj - 4) * 128:(cj - 3) * 128], in_=u[:, T + 2 - 128:T + 2],
                                                                 identity=ident_f[:]),
                              reads=ur + ["ident_f"], writes=[("ps", 6 if cj < 4 else 7)])
                    if has0:
                        P.add("pe", lambda e, cj=cj: e.transpose(out=ps[:, 6, cj * 128:(cj + 1) * 128] if cj < 4 else
                                                                 ps[:, 7, (cj - 4) * 128:(cj - 3) * 128], in_=u[:, 2:130],
                                                                 identity=ident_f[:]),
                              reads=ur + ["ident_f"], writes=[("ps", 6 if cj < 4 else 7)])
                    P.add("act", lambda e, cj=cj: e.activation(out=uhist[:, cj, :], in_=u[:, T:T + 2], func=AF.Copy),
                          reads=ur, writes=["uhist"])
                    P.add("act", lambda e: e.activation(out=ysq[:, 0:T], in_=acc[:, 0:T], func=AF.Square),
                          reads=["acc"], writes=["ysq"])
                    def ssq_mm(cj=cj):
                        for si, (t0, t1, j0, j1) in enumerate(subs):
                            P.add("pe", lambda e, cj=cj, si=si, t0=t0, t1=t1: e.matmul(
                                ps[:, 4 + si, 0:t1 - t0], lhsT=ones_f[:], rhs=ysq[:, t0:t1], start=(cj == 0), stop=(cj == 7)),
                                reads=["ysq", "ones_f"], writes=[("ps", 4 + si)])
                    pending_ssq.append(ssq_mm)
                    P.add("act", lambda e, cj=cj: e.activation(out=ygT[:, cj, 0:T], in_=acc[:, 0:T], func=AF.Copy,
                                                               scale=gaT[:, 1, cj:cj + 1]),
                          reads=["acc", "gaT"], writes=[("hy", cj, 0), ("hy", cj, 1)])
            while pending_ssq:
                pending_ssq.pop(0)()
            if has16 and "conv" in OPT:
                P.add("dve", lambda e: e.tensor_copy(out=attn_tok[96:128, :], in_=ps[96:128, 6:8, :].rearrange("p a b -> p (a b)")),
                      reads=[("ps", 6), ("ps", 7)], writes=["attn_tok"])
                dma("sp", cp_o, attn_tok[126:128, :], "ocv", reads=["attn_tok"], writes=[("outd", "cp")])
            if has0 and "conv" in OPT:
                P.add("dve", lambda e: e.tensor_copy(out=attn_tok[0:16, :], in_=ps[0:16, 6:8, :].rearrange("p a b -> p (a b)")),
                      reads=[("ps", 6), ("ps", 7)], writes=["attn_tok"])
                dma("sp", cs_o[:, 1, :], attn_tok[0:16, :], "ocv", reads=["attn_tok"], writes=[("outd", "cs")])
            for si, (t0, t1, j0, j1) in enumerate(subs if "mrc" in OPT else []):
                P.add("act", lambda e, si=si, t0=t0, t1=t1: e.activation(out=ysq[:, t0:t1], in_=ps[:, 4 + si, 0:t1 - t0], func=AF.Ln,
                                                                         scale=1.0 / 1024, bias=EPS),
                      reads=[("ps", 4 + si)], writes=["ysq"])
                P.add("act", lambda e, t0=t0, t1=t1: e.activation(out=ysq[:, t0:t1], in_=ysq[:, t0:t1], func=AF.Exp, scale=-0.5),
                      reads=["ysq"], writes=["ysq"])
            for j in range(nb if "mrc" in OPT else 0):
                rb = 3 - (j % 2)
                P.add("pe", lambda e, j=j, rb=rb: e.transpose(out=ps[:, rb, 0:32], in_=ysq[0:32, j * 128:(j + 1) * 128], identity=ident_f[0:32, 0:32]),
                      reads=["ysq", "ident_f"], writes=[("ps", rb)])
                P.add("dve", lambda e, j=j, rb=rb: e.tensor_copy(out=rc[:, j:j + 1], in_=ps[:, rb, 0:1]),
                      reads=[("ps", rb)], writes=[("rc", j)])

            units = [(kv, hf) for kv in range(4) for hf in range(2)]
            orr = [("ps", 4), ("ps", 5), ("ps", 6), ("ps", 7)]

            def binfo(j):
                b = blks[j]
                return ([1] if b == 0 else [0, 1]), (2 if b == 0 else (1 if b == 1 else 0)), (256 if b == 0 else 0), (b == 0)

            def emit_S(j, n):
                kv, hf = units[n]
                kbs, mi, c0m, opened = binfo(j)
                sb_ = (j * 8 + n) % 3

                def smm(e):
                    ins = None
                    for kb in kbs:
                        kc0 = (j + kb) * 128
                        ins = e.matmul(ps[:, sb_, kb * 256:(kb + 1) * 256].rearrange("p (a b) -> p a b", a=2),
                                       lhsT=kT[hf * 64:(hf + 1) * 64, kv, kc0:kc0 + 128],
                                       rhs=qT[hf * 64:(hf + 1) * 64, 2 * kv:2 * kv + 2, j * 128:(j + 1) * 128],
                                       start=True, stop=True)
                    return ins
                sj = 0 if j < 3 else 1
                P.add("pe", smm, reads=[("kT", j + kb) for kb in kbs] + [("qT", 2 * kv, sj), ("qT", 2 * kv + 1, sj)],
                      writes=[("ps", sb_)])

            def emit_expmask(j, n):
                kbs, mi, c0m, opened = binfo(j)
                sb_ = (j * 8 + n) % 3
                pb = (j * 8 + n) % 2
                P.add("act", lambda e: e.activation(out=Pt[:, pb, c0m:512], in_=ps[:, sb_, c0m:512], func=AF.Exp),
                      reads=[("ps", sb_)], writes=[("Pt", pb)])
                P.add("dve", lambda e: e.tensor_tensor(
                    out=Pt[:, pb, c0m:512], in0=Pt[:, pb, c0m:512],
                    in1=masks[:, mi].rearrange("p a b -> p (a b)")[:, c0m:512], op=ALU.mult),
                    reads=[("Pt", pb), "masks"], writes=[("Pt", pb)])

            def emit_PV(j, n):
                kv, hf = units[n]
                kbs, mi, c0m, opened = binfo(j)
                pb = (j * 8 + n) % 2

                def pvm(e):
                    ins = None
                    for ci in range(2):
                        hq = 4 * kv + 2 * ci + hf
                        for n_, kb in enumerate(kbs):
                            ins = e.matmul(o_view[:, hq, 0:65], lhsT=Pt[:, pb, kb * 256 + ci * 128:kb * 256 + (ci + 1) * 128],
                                           rhs=vaug[:, j + kb, kv, :],
                                           start=(False if opened else n_ == 0),
                                           stop=(False if opened else n_ == len(kbs) - 1))
                    return ins
                P.add("pe", pvm, reads=[("Pt", pb), "vaug_ones"] + [("vaug", j + kb) for kb in kbs],
                      writes=[("ps", 4 + kv)])

            def emit_tail_dve(j):
                P.add("dve", lambda e: e.tensor_tensor(out=den[:], in0=o_view[:, :, 64], in1=esink[:], op=ALU.add),
                      reads=orr + ["esink"], writes=["den"])
                P.add("dve", lambda e: e.reciprocal(out=rden[:], in_=den[:]), reads=["den"], writes=["rden"])
                P.add("dve", lambda e: e.tensor_tensor(out=attn_tok[:].rearrange("p (h d) -> p h d", h=16), in0=o_view[:, :, 0:64],
                                                       in1=rden[:].unsqueeze(2).broadcast_to([128, 16, 64]), op=ALU.mult),
                      reads=orr + ["rden"], writes=["attn_tok"])

            def emit_tail_act(j):
                P.add("act", lambda e: e.activation(out=xn_tok[:, 0, 0:1024], in_=attn_tok[:], func=AF.Square, accum_out=st_ssa[:, 0:1]),
                      reads=["attn_tok"], writes=[("xn_tok", 0), "st_ssa"])
                P.add("act", lambda e: e.activation(out=st_lna[:, 0:1], in_=st_ssa[:, 0:1], func=AF.Ln, scale=1.0 / 1024, bias=EPS),
                      reads=["st_ssa"], writes=["st_lna"])
                P.add("act", lambda e: e.activation(out=st_ra[:, 0:1], in_=st_lna[:, 0:1], func=AF.Exp, scale=-0.5),
                      reads=["st_lna"], writes=["st_ra"])
                P.add("act", lambda e: e.activation(out=xn_tok[:, 0, 0:1024], in_=attn_tok[:], func=AF.Copy, scale=st_ra[:, 0:1]),
                      reads=["attn_tok", "st_ra"], writes=[("xn_tok", 0)])

            def emit_tail_pe(j):
                tpa = ps_bf(3, 1)

                def tra(e):
                    ins = None
                    for c in range(8):
                        ins = e.transpose(out=tpa[:, c * 128:(c + 1) * 128], in_=xn_tok[:, 0, c * 128:(c + 1) * 128], identity=ident_bf[:])
                    return ins
                P.add("pe", tra, reads=[("xn_tok", 0), "ident_bf"], writes=[("ps", 3)])
                P.add("dve", lambda e: e.tensor_tensor(
                    out=xnT[:, 0:8, j * 128:(j + 1) * 128], in0=tpa.rearrange("p (k t) -> p k t", k=8),
                    in1=gaT[:, 0, :].unsqueeze(2).broadcast_to([128, 8, 128]), op=ALU.mult),
                    reads=[("ps", 3), "gaT"], writes=[("xnT", j)])

            def emit_samples():
                ns = 16 if "samples" in OPT else 0
                tpk = ps_bf(3, 1)

                def s_loadK(i):
                    bf = i % 2
                    o1 = dma("pool", Kd[:, bf, :, 0, :], ck[i].rearrange("j (k d) -> j k d", k=4), "ck%d" % bf, writes=[("Kd", bf)])
                    o2 = dma("pool", Kd[:, bf, :, 1, :], ck[i].rearrange("j (k d) -> j k d", k=4), "ck%d" % bf, writes=[("Kd", bf)])
                    P.group([o1, o2])

                def s_loadV(i):
                    dma("pool", vaugS[:, i % 3, :, 0:64], cv[i].rearrange("j (k d) -> j k d", k=4), "cv%d" % (i % 3), writes=[("vaugS", i % 3)])

                def s_trk(i):
                    bf = i % 2

                    def trk(e):
                        ins = None
                        for kv in range(4):
                            ins = e.transpose(out=tpk[:, kv * 128:(kv + 1) * 128],
                                              in_=Kd[:, bf, kv].rearrange("p a b -> p (a b)"), identity=ident_bf[:])
                        return ins
                    P.add("pe", trk, reads=[("Kd", bf), "ident_bf"], writes=[("ps", 3)])
                    P.add("dve", lambda e: e.tensor_copy(out=KTs[:, bf].rearrange("p a b -> p (a b)"), in_=tpk[:, 0:512]),
                          reads=[("ps", 3)], writes=[("KTs", bf)])

                def s_score(i):
                    bf = i % 2
                    for hf in range(2):
                        sbk = (2 * i + hf) % 3

                        def ssm(e, hf=hf, sbk=sbk):
                            ins = None
                            for kv in range(4):
                                ins = e.matmul(ps[:, sbk, 2 * kv:2 * kv + 2].rearrange("p (a b) -> p a b", a=2),
                                               lhsT=KTs[hf * 64:(hf + 1) * 64, bf, kv, :],
                                               rhs=qT[hf * 64:(hf + 1) * 64, 2 * kv:2 * kv + 2, i:i + 1],
                                               start=True, stop=True)
                            return ins
                        P.add("pe", ssm, reads=[("KTs", bf)] + [("qT", c, 0) for c in range(8)], writes=[("ps", sbk)])
                    P.add("dve", lambda e: e.memset(Pz[:, bf], 0.0), writes=[("Pz", bf)])
                    for hf in range(2):
                        sbk = (2 * i + hf) % 3
                        P.add("act", lambda e, hf=hf, sbk=sbk: e.activation(
                            out=Pz[:, bf].rearrange("p (kv ci hf) t -> p hf kv ci t", kv=4, ci=2, hf=2)[:, hf, :, :, i],
                            in_=ps[:, sbk, 0:8].rearrange("p (kv ci) -> p kv ci", kv=4, ci=2), func=AF.Exp),
                            reads=[("ps", sbk), ("Pz", bf)], writes=[("Pz", bf)])

                def s_pv(i):
                    bf = i % 2

                    def spv(e):
                        ins = None
                        for hq in range(16):
                            ins = e.matmul(o_view[0:16, hq, 0:65], lhsT=Pz[:, bf, hq, :], rhs=vaugS[:, i % 3, hq // 4, :],
                                           start=False, stop=False)
                        return ins
                    P.add("pe", spv, reads=[("Pz", bf), ("vaugS", i % 3), "vaugS_ones"], writes=orr)
                if ns:
                    s_loadK(0)
                    s_loadV(0)
                    s_loadK(1)
                    s_trk(0)
                for i in range(ns):
                    if i + 2 < ns:
                        s_loadK(i + 2)
                    if i + 1 < ns:
                        s_loadV(i + 1)
                        s_trk(i + 1)
                    s_score(i)
                    if i >= 1:
                        s_pv(i - 1)
                if ns:
                    s_pv(ns - 1)

            if "attn" in OPT:
                for n in range(3):
                    emit_S(0, n)
                emit_expmask(0, 0)
                emit_expmask(0, 1)
                for j, b in enumerate(blks):
                    if b == 0:
                        def opn(e):
                            ins = None
                            for i in range(4):
                                ins = e.matmul(ps[:, 4 + i, :], lhsT=zeros_bf[:], rhs=masks[:, 0].rearrange("p a b -> p (a b)"), start=True, stop=False)
                            return ins
                        P.add("pe", opn, reads=["zeros_bf", "masks"], writes=orr)
                    for n in range(8):
                        emit_PV(j, n)
                        if n + 3 < 8:
                            emit_S(j, n + 3)
                        if n + 2 < 8:
                            emit_expmask(j, n + 2)
                        if n == 1 and j > 0:
                            emit_tail_pe(j - 1)
                    if b == 0:
                        emit_samples()

                        def cls(e):
                            ins = None
                            for i in range(4):
                                ins = e.matmul(ps[:, 4 + i, :], lhsT=zeros_bf[:], rhs=masks[:, 0].rearrange("p a b -> p (a b)"), start=False, stop=True)
                            return ins
                        P.add("pe", cls, reads=["zeros_bf", "masks"], writes=orr)
                    emit_tail_dve(j)
                    if j + 1 < nb:
                        for n in range(3):
                            emit_S(j + 1, n)
                        emit_expmask(j + 1, 0)
                        emit_expmask(j + 1, 1)
                    emit_tail_act(j)
                emit_tail_pe(nb - 1)

            dma("sp", gfin_b, gfin.partition_broadcast(128).rearrange("p a b -> p (a b)"), "gfin",
                writes=[("qT", c, s_) for c in range(8) for s_ in range(2)])

            for gi in range(4 if "outproj" in OPT else 0):
                wb = wd_load(wout, gi * G)
                for j in range(nb):
                    sj = 0 if j < 3 else 1
                    for hf in range(2):
                        bk = 4 + 2 * hf

                        def omm(e, j=j, hf=hf, bk=bk, gi=gi, wb=wb):
                            ins = None
                            for dt in range(2):
                                col = (2 * hf + dt) * 512
                                for c in range(G):
                                    if gi < 2:
                                        l = xnT[:, gi * 4 + c, j * 128:(j + 1) * 128]
                                    else:
                                        l = ygT[:, (gi - 2) * 4 + c, j * 128:(j + 1) * 128]
                                    ins = e.matmul(ps[:, bk + dt, :], lhsT=l, rhs=wd[:, wb, c, col:col + 512],
                                                   start=(c == 0), stop=(c == G - 1))
                            return ins
                        rd = [("xnT", j)] if gi < 2 else [("hy", (gi - 2) * 4 + c, sj) for c in range(G)]
                        P.add("pe", omm, reads=rd + [("wd", wb)], writes=[("ps", bk), ("ps", bk + 1)])
                        if gi < 2:
                            P.add("dve", lambda e, j=j, hf=hf, bk=bk: e.tensor_tensor(
                                out=x[:, j, hf * 1024:(hf + 1) * 1024], in0=ps[:, bk:bk + 2, :].rearrange("p a b -> p (a b)"),
                                in1=x[:, j, hf * 1024:(hf + 1) * 1024], op=ALU.add),
                                reads=[("ps", bk), ("ps", bk + 1), ("x", j)], writes=[("x", j)])
                        else:
                            P.add("dve", lambda e, j=j, hf=hf, bk=bk: e.scalar_tensor_tensor(
                                out=x[:, j, hf * 1024:(hf + 1) * 1024], in0=ps[:, bk:bk + 2, :].rearrange("p a b -> p (a b)"),
                                scalar=rc[:, j:j + 1], in1=x[:, j, hf * 1024:(hf + 1) * 1024], op0=ALU.mult, op1=ALU.add),
                                reads=[("ps", bk), ("ps", bk + 1), ("x", j), ("rc", j)], writes=[("x", j)])

            return early

        def final_out(blks):
            nb = len(blks)
            for j in range(nb):
                P.add("act", lambda e, j=j: e.activation(out=xn_tok[:, j % 2, :], in_=x[:, j, :], func=AF.Square,
                                                         accum_out=st_ss[:, j:j + 1]),
                      reads=[("x", j)], writes=[("xn_tok", j % 2), ("ss", j)])
            P.add("act", lambda e: e.activation(out=st_ln[:, 0:nb], in_=st_ss[:, 0:nb], func=AF.Ln, scale=1.0 / D, bias=EPS),
                  reads=[("ss", j) for j in range(nb)], writes=["st_ln"])
            P.add("act", lambda e: e.activation(out=st_r[:, 0:nb], in_=st_ln[:, 0:nb], func=AF.Exp, scale=-0.5),
                  reads=["st_ln"], writes=["st_r"])
            gq = [("qT", c, s_) for c in range(8) for s_ in range(2)]
            for j, b in enumerate(blks):
                P.add("dve", lambda e, j=j: e.scalar_tensor_tensor(out=x[:, j, :], in0=x[:, j, :], scalar=st_r[:, j:j + 1],
                                                                   in1=gfin_b, op0=ALU.mult, op1=ALU.mult),
                      reads=[("x", j), "st_r"] + gq, writes=[("x", j)])
                if b == 0:
                    dma("sp", y_s, x[0:16, j, :], "o%d" % j, reads=[("x", j)], writes=[("outd", "y", b)])
                else:
                    dma("sp", y_p[(b - 1) * 128:b * 128, :], x[:, j, :], "o%d" % j, reads=[("x", j)], writes=[("outd", "y", b)])


        def load_x_block(j, b, via_pool=True):
            if b == 0:
                o1 = dma("sp", x[0:16, j, :], xs, "x%d" % j, writes=[("x", j)])
                o2 = dma("sp", x[112:128, j, :], meta, "x%d" % j, writes=[("x", j)])
                P.group([o1, o2])
            else:
                if via_pool:
                    dma("pool", x[:, j, :], xp[(b - 1) * 128:b * 128, :], "xq%d" % j, writes=[("x", j)])
                else:
                    dma("sp", x[:, j, :], xp[(b - 1) * 128:b * 128, :], "x%d" % j,
                        reads=([("x", 2)] if j >= 3 else []), writes=[("x", j)])

        def rstd_block(j):
            P.add("act", lambda e, j=j: e.activation(out=xn_tok[:, j % 2, :], in_=x[:, j, :], func=AF.Square,
                                                     accum_out=st_ss[:, j:j + 1]),
                  reads=[("x", j)], writes=[("xn_tok", j % 2), ("ss", j)])
            P.add("act", lambda e, j=j: e.activation(out=st_ln[:, j:j + 1], in_=st_ss[:, j:j + 1], func=AF.Ln, scale=1.0 / D, bias=EPS),
                  reads=[("ss", j)], writes=[("ln", j)])
            P.add("act", lambda e, j=j: e.activation(out=st_r[:, j:j + 1], in_=st_ln[:, j:j + 1], func=AF.Exp, scale=-0.5),
                  reads=[("ln", j)], writes=[("r", j)])

        def norm_block(gi, j):
            rstd_block(j)
            if j % 2 == 0:
                P.add("act", lambda e, j=j: e.activation(out=xn_tok[:, j % 2, :], in_=x[:, j, :], func=AF.Copy,
                                                         scale=st_r[:, j:j + 1]),
                      reads=[("x", j), ("r", j)], writes=[("xn_tok", j % 2)])
            else:
                P.add("dve", lambda e, j=j: e.tensor_scalar(out=xn_tok[:, j % 2, :], in0=x[:, j, :], scalar1=st_r[:, j:j + 1],
                                                            scalar2=None, op0=ALU.mult),
                      reads=[("x", j), ("r", j)], writes=[("xn_tok", j % 2)])
            b0 = 4 + 2 * (j % 2)
            tpv = ps_bf(b0, 2)

            def tr(e, j=j, tpv=tpv):
                ins = None
                for k in range(KC):
                    ins = e.transpose(out=tpv[:, k * 128:(k + 1) * 128], in_=xn_tok[:, j % 2, k * 128:(k + 1) * 128],
                                      identity=ident_bf[:])
                return ins
            P.add("pe", tr, reads=[("xn_tok", j % 2), "ident_bf"], writes=[("ps", b0), ("ps", b0 + 1)])
            P.add("dve", lambda e, j=j, tpv=tpv: e.tensor_tensor(
                out=xnT[:, :, j * 128:(j + 1) * 128], in0=tpv.rearrange("p (k t) -> p k t", k=KC),
                in1=gT[:, gi, :].unsqueeze(2).broadcast_to([128, KC, 128]), op=ALU.mult),
                reads=[("ps", b0), ("ps", b0 + 1), "gT"], writes=[("xnT", j)])

        def final_block(j, b):
            gq = [("qT", c, s_) for c in range(8) for s_ in range(2)]
            rstd_block(j)
            P.add("dve", lambda e, j=j: e.scalar_tensor_tensor(out=x[:, j, :], in0=x[:, j, :], scalar=st_r[:, j:j + 1],
                                                               in1=gfin_b, op0=ALU.mult, op1=ALU.mult),
                  reads=[("x", j), ("r", j)] + gq, writes=[("x", j)])
            if b == 0:
                dma("sp", y_s, x[0:16, j, :], "o%d" % j, reads=[("x", j)], writes=[("outd", "y", b)])
            else:
                dma("sp", y_p[(b - 1) * 128:b * 128, :], x[:, j, :], "o%d" % j, reads=[("x", j)], writes=[("outd", "y", b)])

        def boundary(prev_blks, next_blks, finals_done=False):
            npv = len(prev_blks or [])
            nnx = len(next_blks or [])
            ends = []
            if not finals_done:
                for j in range(npv):
                    final_block(j, prev_blks[j])
            for jn in range(nnx):
                load_x_block(jn, next_blks[jn], via_pool=bool(npv))
                if "ffn1" in OPT:
                    norm_block(0, jn)
                ends.append(len(P.ops))
            return ends

        P.add("dve", lambda e: e.memset(x[:, 0, :], 0.0), writes=[("x", 0)])
        setup()
        def hoist(jobs, ends):
            anchors = [ends[k] for k in range(2, len(ends))]
            if jobs and anchors:
                P.move_after(jobs, anchors)

        ends0 = boundary(None, SUPERS[0])
        for si_, blks in enumerate(SUPERS):
            nb = len(blks)
            if "ffn1" in OPT:
                hoist(ffn(w1g, w1u, w1d, nb), ends0)
            if si_ == 0:
                setup_late()
                if "d2d" in OPT:
                    dma("sp", ks_o[:, 0:127, :], ck[:, 1:128, :], "okv2", writes=[("outd", "ks2")])
                    dma("sp", vs_o[:, 0:127, :], cv[:, 1:128, :], "okv2", writes=[("outd", "vs2")])
                    dma("sp", cs_o[:, 0, :], sc[:, 1, :], "okv2", writes=[("outd", "cs2")])
            if dbg and si_ == 0:
                dbg_dump("d_x1", x[:, 0:2, :], [("x", 0), ("x", 1)])
            if "mixer" in OPT:
                ends1 = norm_stage(1, nb)
                hoist(mixer(si_, blks), ends1)
            else:
                dma("sp", gfin_b, gfin.partition_broadcast(128).rearrange("p a b -> p (a b)"), "gfin",
                    writes=[("qT", c, s_) for c in range(8) for s_ in range(2)])
            if dbg and si_ == 0:
                dbg_dump("d_x2", x[:, 0:2, :], [("x", 0), ("x", 1)])
            if "ffn2" in OPT:
                ends2 = norm_stage(2, nb)
                hoist(ffn(w2g, w2u, w2d, nb, after_block=lambda j, blks=blks: final_block(j, blks[j])), ends2)
            ends0 = boundary(blks, SUPERS[si_ + 1] if si_ + 1 < len(SUPERS) else None, finals_done=("ffn2" in OPT))


        all_out = sorted({t for op in P.ops for t in op.writes if isinstance(t, tuple) and t and t[0] in ("outd", "dbgout")}, key=str)
        P.add("sp", lambda e: e.nop(), reads=all_out, name="final")

        sems = {}

        def semh(key):
            if key not in sems:
                nm = ("s_%s_%s" % key).replace(" ", "")
                sems[key] = es.enter_context(nc.semaphore(nm))
            return sems[key]
        for op in P.ops:
            if op.kind == "dma":
                semh(("d", op.sem))
        for eng in ("pe", "act", "dve", "pool", "sp"):
            semh(("e", eng))
        with nc.Block() as block:
            P.emit(nc, block, semh)
    return nc


_NC_CACHE = {}


def _inputs_for_core(c, a):
    f = np.ascontiguousarray
    return {
        "xp": f(a["x_prompt"][c]),
        "xs": f(a["x_sample"][16 * c:16 * c + 16, 0, :]),
        "ck": f(a["cache_swa_k"][0, 16 * c:16 * c + 16].reshape(16, 128, 256)),
        "cv": f(a["cache_swa_v"][0, 16 * c:16 * c + 16].reshape(16, 128, 256)),
        "sc": f(a["state_conv"][0, 16 * c:16 * c + 16]),
        "meta": a["meta_tokens"],
        "gx": a["_gx"], "gac": a["_gac"], "wcv": a["_wcv"], "sinks": a["_sinks"], "gfin": a["_gfin"],
        "w1g": f(a["w1_gate"][0][:, :NGRP * 512]), "w1u": f(a["w1_up"][0][:, :NGRP * 512]), "w1d": f(a["w1_down"][0][:NGRP * 512]),
        "w2g": f(a["w2_gate"][0][:, :NGRP * 512]), "w2u": f(a["w2_up"][0][:, :NGRP * 512]), "w2d": f(a["w2_down"][0][:NGRP * 512]),
        "win": a["w_in"][0], "wout": a["w_out"][0],
    }


def kernel(**inputs):
    a = {k: np.asarray(v) for k, v in inputs.items()}
    f = np.ascontiguousarray
    a["_gx"] = f(np.stack([a["g_ffn1"][0], a["g_mix"][0], a["g_ffn2"][0]]).reshape(3, 16, 128))
    a["_gac"] = f(np.stack([a["g_attn_out"][0], a["g_conv_out"][0]]).reshape(2, 8, 128))
    a["_wcv"] = f(a["w_conv"][0].reshape(24, 128))
    a["_sinks"] = f(a["attn_sinks"].reshape(1, 16))
    a["_gfin"] = f(a["g_final"].reshape(1, D))
    if "nc" not in _NC_CACHE:
        _NC_CACHE["nc"] = build_program()
    nc = _NC_CACHE["nc"]
    in_maps = [_inputs_for_core(c, a) for c in range(NCORES)]
    res = run_bass_kernel_spmd(nc, in_maps, core_ids=list(range(NCORES)))
    R = res.results
    y_prompt = np.stack([R[c]["y_p"] for c in range(NCORES)]).astype(np.float32)
    y_sample = np.concatenate([R[c]["y_s"] for c in range(NCORES)]).reshape(128, 1, D).astype(np.float32)
    kp = np.stack([R[c]["kp_o"] for c in range(NCORES)]).reshape(1, 8, 128, 4, 64).astype(np.float32)
    vp = np.stack([R[c]["vp_o"] for c in range(NCORES)]).reshape(1, 8, 128, 4, 64).astype(np.float32)
    cp = np.stack([R[c]["cp_o"] for c in range(NCORES)]).reshape(1, 8, 2, 1024).astype(np.float32)
    ks = np.concatenate([R[c]["ks_o"] for c in range(NCORES)]).reshape(1, 128, 128, 4, 64).astype(np.float32)
    vs = np.concatenate([R[c]["vs_o"] for c in range(NCORES)]).reshape(1, 128, 128, 4, 64).astype(np.float32)
    cs = np.concatenate([R[c]["cs_o"] for c in range(NCORES)]).reshape(1, 128, 2, 1024).astype(np.float32)
    return (y_prompt, y_sample, kp, vp, cp, ks, vs, cs)
```
